# Optimizing a Trainium2 kernel written in Bass

```python
import jax
import jax.numpy as jnp
from jax import lax
import numpy as np

D_MODEL = 2048
BATCH = 4
SEQ = 4096
DEPTH = 4

N_META = 16
EPS = 1e-6
NEG = -1e30
ATT_BLOCK = 128

SWA_HEADS = 16
SWA_KV_HEADS = 4
SWA_HEAD_DIM = 64
SWA_WINDOW = 128

MLA_HEADS = 16
MLA_Q_RANK = 512
MLA_KV_RANK = 256
MLA_NOPE_DIM = 64
MLA_ROPE_DIM = 32
MLA_V_DIM = 64
ROPE_THETA = 10000.0

REC_EXPAND = 128
REC_HEADS = D_MODEL // REC_EXPAND
REC_VDIM = D_MODEL // REC_HEADS
REC_CHUNK = 64

D_FF = 5632
CONV_WIDTH = 3

N_ATT_LAYERS = (DEPTH + 1) // 2
N_REC_LAYERS = DEPTH // 2

SWA_Q = SWA_HEADS * SWA_HEAD_DIM
SWA_KV = SWA_KV_HEADS * SWA_HEAD_DIM
ATT_IN = SWA_Q + 2 * SWA_KV + MLA_Q_RANK + MLA_KV_RANK + MLA_ROPE_DIM
ATT_OUT = SWA_Q + MLA_HEADS * MLA_V_DIM
REC_K = REC_HEADS * REC_EXPAND
REC_V = REC_HEADS * REC_VDIM
REC_IN = 2 * REC_K + 2 * REC_V

kernel_name = 'hybrid_swa_mla_hgrn2_convffn_meta'


def rms_norm(x, g):
    xf = x.astype(jnp.float32)
    y = xf * lax.rsqrt(jnp.mean(xf * xf, axis=-1, keepdims=True) + EPS)
    return (y * g.astype(jnp.float32)).astype(x.dtype)


def pad_front(a, block):
    pad = (-N_META) % block
    widths = [(0, 0), (pad, 0)] + [(0, 0)] * (a.ndim - 2)
    return jnp.pad(a, widths), pad


def apply_rope(x, cos, sin):
    x1, x2 = jnp.split(x.astype(jnp.float32), 2, axis=-1)
    shape = (cos.shape[0],) + (1,) * (x.ndim - 3) + (cos.shape[1],)
    c, s = cos.reshape(shape), sin.reshape(shape)
    return jnp.concatenate([x1 * c - x2 * s, x2 * c + x1 * s], axis=-1).astype(x.dtype)


def sliding_window_sink_attention(q, k, v, sinks):
    B = q.shape[0]
    G = SWA_HEADS // SWA_KV_HEADS
    blk = ATT_BLOCK
    qp, pad = pad_front(q, blk)
    kp, _ = pad_front(k, blk)
    vp, _ = pad_front(v, blk)
    T = qp.shape[1]
    NB = T // blk
    qb = qp.reshape(B, NB, blk, SWA_KV_HEADS, G, SWA_HEAD_DIM)

    def band(a):
        ab = a.reshape(B, NB, blk, SWA_KV_HEADS, SWA_HEAD_DIM)
        prev = jnp.concatenate([jnp.zeros_like(ab[:, :1]), ab[:, :-1]], axis=1)
        return jnp.concatenate([prev, ab], axis=2)

    kb, vb = band(kp), band(vp)
    s = jnp.einsum('bnqkgd,bnskd->bnkgqs', qb, kb, preferred_element_type=jnp.float32) * (SWA_HEAD_DIM ** -0.5)
    qi = jnp.arange(blk)[:, None]
    kj = jnp.arange(2 * blk)[None, :]
    rel = qi + blk - kj
    in_window = (rel >= 0) & (rel < SWA_WINDOW)
    key_pos = jnp.arange(NB)[:, None] * blk + jnp.arange(2 * blk)[None, :] - blk
    key_ok = key_pos >= pad
    mask = in_window[None] & key_ok[:, None, :]
    s = jnp.where(mask[None, :, None, None], s, NEG)
    sink = sinks.astype(jnp.float32).reshape(SWA_KV_HEADS, G)[None, None, :, :, None, None]
    m = jnp.maximum(jnp.max(s, axis=-1, keepdims=True), sink)
    p = jnp.exp(s - m)
    p = p / (jnp.sum(p, axis=-1, keepdims=True) + jnp.exp(sink - m))
    o = jnp.einsum('bnkgqs,bnskd->bnqkgd', p.astype(vb.dtype), vb)
    return o.reshape(B, T, SWA_Q)[:, pad:]


def latent_attention(c_q, c_kv, k_rope, q_norm, w_uq, kv_norm, w_ukv, cos, sin):
    B, L = c_q.shape[:2]
    blk = ATT_BLOCK
    q = (rms_norm(c_q, q_norm) @ w_uq).reshape(B, L, MLA_HEADS, MLA_NOPE_DIM + MLA_ROPE_DIM)
    q_nope = q[..., :MLA_NOPE_DIM]
    q_rope = apply_rope(q[..., MLA_NOPE_DIM:], cos, sin)
    kv = (rms_norm(c_kv, kv_norm) @ w_ukv).reshape(B, L, MLA_HEADS, MLA_NOPE_DIM + MLA_V_DIM)
    k_nope = kv[..., :MLA_NOPE_DIM]
    v = kv[..., MLA_NOPE_DIM:]
    k_rope = apply_rope(k_rope, cos, sin)
    q_nope, pad = pad_front(q_nope, blk)
    q_rope, _ = pad_front(q_rope, blk)
    k_nope, _ = pad_front(k_nope, blk)
    k_rope, _ = pad_front(k_rope, blk)
    v, _ = pad_front(v, blk)
    T = q_nope.shape[1]
    NB = T // blk
    scale = (MLA_NOPE_DIM + MLA_ROPE_DIM) ** -0.5
    key_pos = jnp.arange(T)

    def query_block(args):
        qn, qr, n = args
        s = (jnp.einsum('bqhd,bshd->bhqs', qn, k_nope, preferred_element_type=jnp.float32)
             + jnp.einsum('bqhd,bsd->bhqs', qr, k_rope, preferred_element_type=jnp.float32)) * scale
        q_pos = n * blk + jnp.arange(blk)
        mask = (key_pos[None, :] <= q_pos[:, None]) & (key_pos[None, :] >= pad)
        p = jax.nn.softmax(jnp.where(mask, s, NEG), axis=-1)
        return jnp.einsum('bhqs,bshd->bqhd', p.astype(v.dtype), v)

    qn_b = q_nope.reshape(B, NB, blk, MLA_HEADS, MLA_NOPE_DIM).transpose(1, 0, 2, 3, 4)
    qr_b = q_rope.reshape(B, NB, blk, MLA_HEADS, MLA_ROPE_DIM).transpose(1, 0, 2, 3, 4)
    o = lax.map(query_block, (qn_b, qr_b, jnp.arange(NB)))
    return o.transpose(1, 0, 2, 3, 4).reshape(B, T, MLA_HEADS * MLA_V_DIM)[:, pad:]


def attention_mixer(h, w_in, sinks, q_norm, w_uq, kv_norm, w_ukv, w_out, cos, sin):
    B, L, _ = h.shape
    z = h @ w_in
    cuts = [SWA_Q, SWA_Q + SWA_KV, SWA_Q + 2 * SWA_KV, SWA_Q + 2 * SWA_KV + MLA_Q_RANK,
            SWA_Q + 2 * SWA_KV + MLA_Q_RANK + MLA_KV_RANK]
    q_a, k_a, v_a, c_q, c_kv, k_r = jnp.split(z, cuts, axis=-1)
    o_a = sliding_window_sink_attention(
        q_a.reshape(B, L, SWA_HEADS, SWA_HEAD_DIM),
        k_a.reshape(B, L, SWA_KV_HEADS, SWA_HEAD_DIM),
        v_a.reshape(B, L, SWA_KV_HEADS, SWA_HEAD_DIM), sinks)
    o_b = latent_attention(c_q, c_kv, k_r, q_norm, w_uq, kv_norm, w_ukv, cos, sin)
    return jnp.concatenate([o_a, o_b], axis=-1) @ w_out


def hgrn2_mixer(h, w_in, lower_bound, out_norm, w_out):
    B, L, _ = h.shape
    z = h @ w_in
    q, f, i, g = jnp.split(z, [REC_K, 2 * REC_K, 2 * REC_K + REC_V], axis=-1)
    lb = lower_bound.astype(jnp.float32)
    log_f = jnp.logaddexp(jnp.log(lb), jnp.log1p(-lb) + jax.nn.log_sigmoid(f.astype(jnp.float32)))
    k = 1.0 - jnp.exp(log_f)
    q = jax.nn.silu(q.astype(jnp.float32))

    def heads(a, d):
        return a.reshape(B, L, REC_HEADS, d)

    qp, pad = pad_front(heads(q, REC_EXPAND), REC_CHUNK)
    kp, _ = pad_front(heads(k, REC_EXPAND), REC_CHUNK)
    gp, _ = pad_front(heads(log_f, REC_EXPAND), REC_CHUNK)
    vp, _ = pad_front(heads(i.astype(jnp.float32), REC_VDIM), REC_CHUNK)
    T = qp.shape[1]
    NC = T // REC_CHUNK

    def to_chunks(a):
        return a.reshape(B, NC, REC_CHUNK, REC_HEADS, a.shape[-1]).transpose(1, 0, 3, 2, 4)

    causal = jnp.tril(jnp.ones((REC_CHUNK, REC_CHUNK), dtype=bool))[:, :, None]

    def chunk_step(S, inp):
        qc, kc, vc, gc = inp
        b = jnp.cumsum(gc, axis=2)
        o_inter = jnp.einsum('bhtk,bhkv->bhtv', qc * jnp.exp(b), S)
        diff = b[:, :, :, None, :] - b[:, :, None, :, :]
        decay = jnp.exp(jnp.where(causal, diff, -jnp.inf))
        att = jnp.einsum('bhtk,bhtsk,bhsk->bhts', qc, decay, kc)
        o = o_inter + jnp.einsum('bhts,bhsv->bhtv', att, vc)
        b_last = b[:, :, -1:, :]
        S = (jnp.exp(b_last[:, :, 0, :])[..., None] * S
             + jnp.einsum('bhsk,bhsv->bhkv', kc * jnp.exp(b_last - b), vc))
        return S, o

    S0 = jnp.zeros((B, REC_HEADS, REC_EXPAND, REC_VDIM), jnp.float32)
    _, o = lax.scan(chunk_step, S0, (to_chunks(qp), to_chunks(kp), to_chunks(vp), to_chunks(gp)))
    o = o.transpose(1, 0, 3, 2, 4).reshape(B, T, REC_HEADS, REC_VDIM)[:, pad:]
    o = rms_norm(o, out_norm) * jax.nn.silu(heads(g.astype(jnp.float32), REC_VDIM))
    return o.reshape(B, L, REC_V).astype(h.dtype) @ w_out


def conv_ffn(h, w_up, w_gate, conv_w, conv_b, w_down):
    u = h @ w_up
    a = h @ w_gate
    a = lax.conv_general_dilated(
        a, conv_w[:, None, :], window_strides=(1,), padding=[(CONV_WIDTH - 1, 0)],
        dimension_numbers=('NWC', 'WIO', 'NWC'), feature_group_count=D_FF) + conv_b
    return (jax.nn.silu(a) * u) @ w_down


def setup_inputs(seed: int = 0) -> dict:
    key = jax.random.key(seed)
    ks = jax.random.split(key, 24)
    f32 = jnp.float32

    def w(k, shape, fan_in):
        return jax.random.normal(k, shape, f32) * (fan_in ** -0.5)

    def gain(k, shape):
        return 1.0 + 0.02 * jax.random.normal(k, shape, f32)

    NA, NR = N_ATT_LAYERS, N_REC_LAYERS
    return {
        'x': jax.random.normal(ks[0], (BATCH, SEQ, D_MODEL), f32),
        'meta_tokens': jax.random.normal(ks[1], (N_META, D_MODEL), f32),
        'mix_norm': gain(ks[2], (DEPTH, D_MODEL)),
        'ffn_norm': gain(ks[3], (DEPTH, D_MODEL)),
        'final_norm': gain(ks[4], (D_MODEL,)),
        'att_w_in': w(ks[5], (NA, D_MODEL, ATT_IN), D_MODEL),
        'att_sinks': 0.5 * jax.random.normal(ks[6], (NA, SWA_HEADS), f32),
        'mla_q_norm': gain(ks[7], (NA, MLA_Q_RANK)),
        'mla_w_uq': w(ks[8], (NA, MLA_Q_RANK, MLA_HEADS * (MLA_NOPE_DIM + MLA_ROPE_DIM)), MLA_Q_RANK),
        'mla_kv_norm': gain(ks[9], (NA, MLA_KV_RANK)),
        'mla_w_ukv': w(ks[10], (NA, MLA_KV_RANK, MLA_HEADS * (MLA_NOPE_DIM + MLA_V_DIM)), MLA_KV_RANK),
        'att_w_out': w(ks[11], (NA, ATT_OUT, D_MODEL), ATT_OUT),
        'rec_w_in': w(ks[12], (NR, D_MODEL, REC_IN), D_MODEL),
        'rec_lower_bounds': jax.random.normal(ks[13], (NR, REC_K), f32),
        'rec_out_norm': gain(ks[14], (NR, REC_VDIM)),
        'rec_w_out': w(ks[15], (NR, REC_V, D_MODEL), REC_V),
        'ffn_w_up': w(ks[16], (DEPTH, D_MODEL, D_FF), D_MODEL),
        'ffn_w_gate': w(ks[17], (DEPTH, D_MODEL, D_FF), D_MODEL),
        'ffn_conv_w': w(ks[18], (DEPTH, CONV_WIDTH, D_FF), CONV_WIDTH),
        'ffn_conv_b': 0.01 * jax.random.normal(ks[19], (DEPTH, D_FF), f32),
        'ffn_w_down': w(ks[20], (DEPTH, D_FF, D_MODEL), D_FF),
    }


def reference(x, meta_tokens, mix_norm, ffn_norm, final_norm, att_w_in, att_sinks, mla_q_norm,
              mla_w_uq, mla_kv_norm, mla_w_ukv, att_w_out, rec_w_in, rec_lower_bounds, rec_out_norm,
              rec_w_out, ffn_w_up, ffn_w_gate, ffn_conv_w, ffn_conv_b, ffn_w_down):
    B = x.shape[0]
    meta = jnp.broadcast_to(meta_tokens[None].astype(x.dtype), (B, N_META, D_MODEL))
    hs = jnp.concatenate([meta, x], axis=1)
    L = hs.shape[1]
    half = MLA_ROPE_DIM // 2
    inv_freq = ROPE_THETA ** (-2.0 * jnp.arange(half, dtype=jnp.float32) / MLA_ROPE_DIM)
    ang = jnp.arange(L).astype(jnp.float32)[:, None] * inv_freq[None, :]
    cos, sin = jnp.cos(ang), jnp.sin(ang)
    sm = jax.nn.softmax(rec_lower_bounds.astype(jnp.float32), axis=0)
    lower = jnp.cumsum(sm.at[0].set(0.0), axis=0)
    for layer in range(DEPTH):
        h = rms_norm(hs, mix_norm[layer])
        if layer % 2 == 0:
            a = layer // 2
            hs = hs + attention_mixer(h, att_w_in[a], att_sinks[a], mla_q_norm[a], mla_w_uq[a],
                                      mla_kv_norm[a], mla_w_ukv[a], att_w_out[a], cos, sin)
        else:
            r = layer // 2
            hs = hs + hgrn2_mixer(h, rec_w_in[r], lower[r], rec_out_norm[r], rec_w_out[r])
        h = rms_norm(hs, ffn_norm[layer])
        hs = hs + conv_ffn(h, ffn_w_up[layer], ffn_w_gate[layer], ffn_conv_w[layer],
                           ffn_conv_b[layer], ffn_w_down[layer])
    return rms_norm(hs, final_norm)[:, N_META:]
```

```python
import numpy as np
import concourse.bass as bass
import concourse.mybir as mybir
from concourse.bass_utils import run_bass_kernel_spmd

F32 = mybir.dt.float32
BF16 = mybir.dt.bfloat16
AF = mybir.ActivationFunctionType
ALU = mybir.AluOpType

D = 2048
KC = 16
NMETA = 16
DFF = 5632
FC = 44
EPS = 1e-6
TT = 512
import os as _os
PIPE_R = _os.environ.get('PIPE_R', '1') == '1'
PIPE_S = _os.environ.get('PIPE_S', '1') == '1'
PIPE_M = _os.environ.get('PIPE_M', '1') == '1'
PACE_ON = _os.environ.get('PACE_ON', '1') == '1'
HCHUNK = _os.environ.get('HCHUNK', '1') == '1'
PSQ_ON = False


class Buf:
    __slots__ = ("name", "arena", "lo", "hi", "last_w", "rd", "rd_dma")

    def __init__(self, name, arena=None, lo=0, hi=1):
        self.name = name
        self.arena = arena if arena is not None else [self]
        if arena is not None:
            arena.append(self)
        self.lo, self.hi = lo, hi
        self.last_w = None
        self.rd = {}
        self.rd_dma = []


class Op:
    __slots__ = ("eng", "fn", "dma", "pos", "signal", "waits", "slot", "val", "cnt")

    def __init__(self, eng, fn, dma):
        self.eng, self.fn, self.dma = eng, fn, dma
        self.pos = 0
        self.signal = False
        self.waits = []
        self.slot = None
        self.val = 0
        self.cnt = 0


ENGS = ("pe", "act", "dve", "pool", "sp")
SEM_CH = 12000
NSLOT = 40


class Prog:
    def __init__(self, nc):
        self.nc = nc
        self.streams = {e: [] for e in ENGS}
        self.waited = {e: {f: -1 for f in ENGS} for e in ENGS}
        self.waited_dma = {e: {} for e in ENGS}
        self.slot_last = {}
        self.ndma = {e: 0 for e in ENGS}
        self.out_dmas = []

    def _overl(self, b):
        if len(b.arena) == 1:
            return b.arena
        return [r for r in b.arena if r.lo < b.hi and b.lo < r.hi]

    def _add(self, eng, fn, r, w, dma):
        op = Op(eng, fn, dma)
        st = self.streams[eng]
        op.pos = len(st)
        deps = {}
        for b in r:
            for q in self._overl(b):
                if q.last_w is not None:
                    deps[id(q.last_w)] = q.last_w
        for b in w:
            for q in self._overl(b):
                if q.last_w is not None:
                    deps[id(q.last_w)] = q.last_w
                for d in q.rd.values():
                    deps[id(d)] = d
                for d in q.rd_dma:
                    deps[id(d)] = d
        if dma:
            slot = (eng, self.ndma[eng] % NSLOT)
            self.ndma[eng] += 1
            prev = self.slot_last.get(slot)
            if prev is not None:
                deps[id(prev)] = prev
            op.slot = slot
            op.val = (prev.val if prev is not None else 0) + 16
            self.slot_last[slot] = op
        wd = self.waited_dma[eng]
        wc = self.waited[eng]
        for d in deps.values():
            if d is op:
                continue
            if d.dma:
                if wd.get(d.slot, 0) >= d.val:
                    continue
                wd[d.slot] = d.val
                op.waits.append(d)
            else:
                if d.eng == eng and (eng == "pe" or dma):
                    if eng == "pe":
                        continue
                if wc[d.eng] >= d.pos:
                    continue
                wc[d.eng] = d.pos
                d.signal = True
                op.waits.append(d)
        for b in r:
            if dma:
                b.rd_dma.append(op)
            else:
                b.rd[eng] = op
        for b in w:
            b.last_w = op
            b.rd = {}
            b.rd_dma = []
        st.append(op)
        return op

    def op(self, eng, fn, r=(), w=()):
        return self._add(eng, fn, r, w, False)

    def dma(self, eng, out, in_, r=(), w=(), is_out=False):
        o = self._add(eng, lambda e: e.dma_start(out=out, in_=in_), r, w, True)
        if is_out:
            self.out_dmas.append(o)
        return o

    def finish(self):
        nc = self.nc
        sems = {}

        def getsem(key):
            if key not in sems:
                sems[key] = nc.alloc_semaphore("s_%s_%s" % (key[0], key[1]))
            return sems[key]

        for e in ENGS:
            c = 0
            for o in self.streams[e]:
                if o.signal and not o.dma:
                    o.cnt = c
                    c += 1
        handles = {"pe": "tensor", "act": "scalar", "dve": "vector", "pool": "gpsimd", "sp": "sync"}
        out_dmas = self.out_dmas

        def emit(ename, e):
            for o in self.streams[ename]:
                for d in o.waits:
                    if d.dma:
                        e.wait_ge(getsem(("d" + d.slot[0], d.slot[1])), d.val)
                    else:
                        e.wait_ge(getsem((d.eng, d.cnt // SEM_CH)), d.cnt % SEM_CH + 1)
                ins = o.fn(e)
                if o.dma:
                    ins.then_inc(getsem(("d" + o.slot[0], o.slot[1])), 16)
                elif o.signal:
                    ins.then_inc(getsem((o.eng, o.cnt // SEM_CH)), 1)
            if ename == "sp":
                for d in out_dmas:
                    e.wait_ge(getsem(("d" + d.slot[0], d.slot[1])), d.val)

        with nc.Block() as block:
            for ename in ENGS:
                getattr(block, handles[ename])(lambda e, _n=ename: emit(_n, e))


def tile_w(W, M):
    K, N = W.shape
    return np.ascontiguousarray(W.reshape(K // 128, 128, N // M, M).transpose(2, 1, 0, 3))


def col_layout(v):
    return np.ascontiguousarray(v.reshape(-1, 128).T)


class Cfg:
    def __init__(self, t_real, depth, mixers=True, layers=None):
        self.t_real = t_real
        self.depth = depth
        self.mixers = mixers
        if layers is None:
            layers = ["a" if (l % 2 == 0) else "r" for l in range(depth)] if mixers else ["n"] * depth
        self.layers = layers
        self.n_att = sum(1 for t in layers if t == "a")
        self.n_rec = sum(1 for t in layers if t == "r")
        self.T = NMETA + t_real
        self.tiles = [(0, NMETA)] + [(NMETA + i * TT, TT) for i in range(t_real // TT)]


def build(cfg):
    nc = bass.Bass("TRN2", target_bir_lowering=False)
    P = Prog(nc)
    L = cfg.depth
    T = cfg.T
    NR = max(cfg.n_rec, 1)
    NA = max(cfg.n_att, 1)

    def din(name, shape, dt=F32):
        return nc.dram_tensor(name, list(shape), dt, kind="ExternalInput").ap()

    def dscr(name, shape, dt=BF16):
        return nc.dram_tensor(name, list(shape), dt, kind="Internal").ap()

    def sb(name, shape, dt=F32):
        return nc.alloc_sbuf_tensor(name, list(shape), dt)

    xT = din("xT", [D, cfg.t_real])
    metaT = din("metaT", [D, NMETA])
    outT = nc.dram_tensor("outT", [D, cfg.t_real], F32, kind="ExternalOutput").ap()
    hsT = dscr("hsT", [D, T], F32)
    gains_d = din("gains", [128, (2 * L + 1) * KC])
    convw_d = din("convw", [128, L * FC * 4])
    consts_d = din("consts", [128, 2048])
    wu_f = [din("wu%d" % l, [FC, 128, KC * 128]) for l in range(L)]
    wg_f = [din("wg%d" % l, [FC, 128, KC * 128]) for l in range(L)]
    wd_f = [din("wd%d" % l, [KC, 128, FC * 128]) for l in range(L)]
    wu_b = [dscr("wub%d" % l, [FC, 128, KC * 128]) for l in range(L)]
    wg_b = [dscr("wgb%d" % l, [FC, 128, KC * 128]) for l in range(L)]
    wd_b = [dscr("wdb%d" % l, [KC, 128, FC * 128]) for l in range(L)]
    rwin_f = [din("rwin%d" % r, [64, 128, KC * 128]) for r in range(cfg.n_rec)]
    rwout_f = [din("rwout%d" % r, [KC, 128, KC * 128]) for r in range(cfg.n_rec)]
    rwin_b = [dscr("rwinb%d" % r, [64, 128, KC * 128]) for r in range(cfg.n_rec)]
    rwout_b = [dscr("rwoutb%d" % r, [KC, 128, KC * 128]) for r in range(cfg.n_rec)]
    lbraw_d = din("lbraw", [128, NR * 16])
    ong_d = din("ong", [128, NR])
    awin_f = [din("awin%d" % a, [18, 128, KC * 128]) for a in range(cfg.n_att)]
    awin_b = [dscr("awinb%d" % a, [18, 128, KC * 128]) for a in range(cfg.n_att)]
    awkr_f = [din("awkr%d" % a, [128, KC * 32]) for a in range(cfg.n_att)]
    awkr_b = [dscr("awkrb%d" % a, [128, KC * 32]) for a in range(cfg.n_att)]
    awkrrot_b = [dscr("awkrrotb%d" % a, [128, KC * 32]) for a in range(cfg.n_att)]
    wuq_f = [din("wuq%d" % a, [4, 128, 4 * 384]) for a in range(cfg.n_att)]
    wuq_b = [dscr("wuqb%d" % a, [4, 128, 4 * 384]) for a in range(cfg.n_att)]
    wrot_b = [dscr("wrotb%d" % a, [128, 2048]) for a in range(cfg.n_att)]
    wukvk_f = [din("wukvk%d" % a, [128, 2048]) for a in range(cfg.n_att)]
    wukvk_b = [dscr("wukvkb%d" % a, [128, 2048]) for a in range(cfg.n_att)]
    wukvv_f = [din("wukvv%d" % a, [128, 2048]) for a in range(cfg.n_att)]
    wukvv_b = [dscr("wukvvb%d" % a, [128, 2048]) for a in range(cfg.n_att)]
    awout_f = [din("awout%d" % a, [KC, 128, KC * 128]) for a in range(cfg.n_att)]
    awout_b = [dscr("awoutb%d" % a, [KC, 128, KC * 128]) for a in range(cfg.n_att)]
    anorm_d = din("anorm", [128, NA * 6])
    sinkc_d = din("sinkc", [128, NA * 8])
    ropec_d = din("ropec", [32, T])
    ropes_d = din("ropes", [32, T])
    kcat_s = [dscr("kcat%d" % a, [16, 96, T]) for a in range(cfg.n_att)]
    vml_s = [dscr("vml%d" % a, [T, 1024]) for a in range(cfg.n_att)]

    X = sb("X", [128, KC, TT]); Xb = [Buf("X%d" % i) for i in range(KC)]
    H = sb("H", [128, KC, TT], BF16); Hb = [Buf("H%d" % i) for i in range(KC)]
    if not HCHUNK:
        Hb = [Hb[0]] * KC
    GA = sb("GA", [128, FC * TT], BF16)
    garena = []
    Gb = [Buf("G%d" % i, garena, i * TT * 2, (i + 1) * TT * 2) for i in range(FC)]

    def Gv(fc, w):
        return GA[:, fc * TT:fc * TT + w]

    def ga_f32(byte_off, n):
        return GA[:, byte_off // 2: byte_off // 2 + 2 * n].bitcast(F32)

    NWA = 4
    WA = [sb("WA%d" % i, [128, KC * 128], BF16) for i in range(NWA)]; WAb = [Buf("WA%d" % i) for i in range(NWA)]
    WD = [sb("WD%d" % i, [128, FC * 128], BF16) for i in range(2)]
    WDar = [[], []]
    WDb = [Buf("WD%d" % i, WDar[i], 0, FC * 128) for i in range(2)]
    WDh = [(WD[i][:, hh * 2048:(hh + 1) * 2048], Buf("WD%d_%d" % (i, hh), WDar[i], hh * 2048, (hh + 1) * 2048))
           for i in range(2) for hh in range(2)]
    SQ = [sb("SQ%d" % i, [128, TT], BF16) for i in range(2)]; SQb = [Buf("SQ%d" % i) for i in range(2)]
    RSTD = sb("RSTD", [128, TT]); RSTDb = Buf("RSTD")
    TMPN = sb("TMPN", [128, TT]); TMPNb = Buf("TMPN")
    ASB = [sb("ASB%d" % i, [128, TT + 2]) for i in range(2)]; ASBb = [Buf("ASB%d" % i) for i in range(2)]
    ACC = [sb("ACC%d" % i, [128, TT]) for i in range(2)]; ACCb = [Buf("ACC%d" % i) for i in range(2)]
    CC = sb("CC", [128, L, FC, 2]); CCb = [Buf("CC%d" % l) for l in range(L)]
    GN = sb("GN", [128, (2 * L + 1) * KC]); GNb = Buf("GN")
    CW = sb("CW", [128, L * FC * 4]); CWb = Buf("CW")
    CONS = sb("CONS", [128, 2048]); CONSb = Buf("CONS")
    TRI2 = CONS[:, 1280:1536]
    STRICT2 = CONS[:, 1536:1792]
    PREV16_2 = CONS[:, 1792:2048]
    MASKBD = CONS[:, 0:128]
    RESET64 = CONS[:, 256:768]
    RESET128 = CONS[:, 768:1280]
    ONES = sb("ONES", [128, 128], BF16); ONESb = Buf("ONES")
    IDENT = sb("IDENT", [128, 128], BF16); IDENTb = Buf("IDENT")
    EPSC = sb("EPSC", [128, 1]); EPSCb = Buf("EPSC")
    ONEC = sb("ONEC", [128, 1]); ONECb = Buf("ONEC")
    PS = [nc.alloc_psum_tensor("PS%d" % i, [128, 512], F32) for i in range(7)]
    PSar = [[] for i in range(7)]
    PSb = [Buf("PS%d" % i, PSar[i], 0, 512) for i in range(7)]
    PSq = [[Buf("PS%d_%d" % (i, q), PSar[i], q * 128, (q + 1) * 128) for q in range(4)] for i in range(7)]
    if not PSQ_ON:
        PSq = [[PSb[i]] * 4 for i in range(7)]
    PST = nc.alloc_psum_tensor("PST", [128, 1024], BF16); PSTb = Buf("PST")

    def MM(out, lhsT, rhs, start, stop, r, w):
        P.op("pe", lambda e: e.matmul(out, lhsT=lhsT, rhs=rhs, start=start, stop=stop), r=r, w=w)

    def ACTF(out, in_, func, r, w, **kw):
        P.op("act", lambda e: e.activation(out=out, in_=in_, func=func, **kw), r=r, w=w)

    def ACOPY(out, in_, r, w):
        P.op("act", lambda e: e.copy(out=out, in_=in_), r=r, w=w)

    def VTT(out, in0, in1, op, r, w):
        P.op("dve", lambda e: e.tensor_tensor(out=out, in0=in0, in1=in1, op=op), r=r, w=w)

    def VTS(out, in0, s1, s2, op0, op1, r, w):
        if op1 is None:
            P.op("dve", lambda e: e.tensor_scalar(out=out, in0=in0, scalar1=s1, scalar2=None, op0=op0), r=r, w=w)
        else:
            P.op("dve", lambda e: e.tensor_scalar(out=out, in0=in0, scalar1=s1, scalar2=s2, op0=op0, op1=op1),
                 r=r, w=w)

    def VSTT(out, in0, scalar, in1, op0, op1, r, w):
        P.op("dve", lambda e: e.scalar_tensor_tensor(out=out, in0=in0, scalar=scalar, in1=in1, op0=op0, op1=op1),
             r=r, w=w)

    def VCOPY(out, in_, r, w):
        P.op("dve", lambda e: e.tensor_copy(out=out, in_=in_), r=r, w=w)

    def VRECIP(out, in_, r, w):
        P.op("dve", lambda e: e.reciprocal(out=out, in_=in_), r=r, w=w)

    def VMEMSET(ap, val, w):
        P.op("dve", lambda e: e.memset(ap, val), w=w)

    P.dma("sp", GN[:, :], gains_d[:, :], w=[GNb])
    P.dma("sp", CW[:, :], convw_d[:, :], w=[CWb])
    P.dma("sp", CONS[:, :], consts_d[:, :], w=[CONSb])
    VMEMSET(ONES[:, :], 1.0, [ONESb])
    VMEMSET(EPSC[:, :], EPS, [EPSCb])
    VMEMSET(ONEC[:, :], 1.0, [ONECb])
    VMEMSET(CC[:, :, :, :], 0.0, CCb)
    VCOPY(IDENT[:, :], CONS[:, 128:256], [CONSb], [IDENTb])

    LBR = sb("LBR", [128, NR * 16]); LBRb = Buf("LBR")
    LBE = sb("LBE", [128, NR * 16]); LBEb = Buf("LBE")
    LB = sb("LB", [128, NR * 16]); LBb = Buf("LB")
    OML = sb("OML", [128, NR * 16]); OMLb = Buf("OML")
    LBM = sb("LBM", [128, 16]); LBMb = Buf("LBM")
    LBS = sb("LBS", [128, 16]); LBSb = Buf("LBS")
    ONG = sb("ONG", [128, NR]); ONGb = Buf("ONG")
    EPSR = sb("EPSR", [128, 1]); EPSRb = Buf("EPSR")
    if cfg.n_rec > 0:
        P.dma("sp", LBR[:, :], lbraw_d[:, :], w=[LBRb])
        P.dma("sp", ONG[:, :], ong_d[:, :], w=[ONGb])
        VCOPY(LBM[:, :], LBR[:, 0:16], [LBRb], [LBMb])
        for r in range(1, NR):
            VTT(LBM[:, :], LBM[:, :], LBR[:, r * 16:(r + 1) * 16], ALU.max, [LBMb, LBRb], [LBMb])
        for r in range(NR):
            VTT(LBE[:, r * 16:(r + 1) * 16], LBR[:, r * 16:(r + 1) * 16], LBM[:, :], ALU.subtract,
                [LBRb, LBMb], [LBEb])
        ACTF(LBE[:, :], LBE[:, :], AF.Exp, [LBEb], [LBEb])
        VCOPY(LBS[:, :], LBE[:, 0:16], [LBEb], [LBSb])
        for r in range(1, NR):
            VTT(LBS[:, :], LBS[:, :], LBE[:, r * 16:(r + 1) * 16], ALU.add, [LBSb, LBEb], [LBSb])
        VRECIP(LBS[:, :], LBS[:, :], [LBSb], [LBSb])
        VMEMSET(LB[:, 0:16], 0.0, [LBb])
        for r in range(1, NR):
            VTT(LBE[:, r * 16:(r + 1) * 16], LBE[:, r * 16:(r + 1) * 16], LBS[:, :], ALU.mult, [LBEb, LBSb], [LBEb])
            VTT(LB[:, r * 16:(r + 1) * 16], LB[:, (r - 1) * 16:r * 16], LBE[:, r * 16:(r + 1) * 16], ALU.add,
                [LBb, LBEb], [LBb])
        VTS(OML[:, :], LB[:, :], -1.0, 1.0, ALU.mult, ALU.add, [LBb], [OMLb])

    wub_b = [[Buf("wub%d_%d" % (l, i)) for i in range(FC)] for l in range(L)]
    wgb_b = [[Buf("wgb%d_%d" % (l, i)) for i in range(FC)] for l in range(L)]
    wdb_b = [[Buf("wdb%d_%d" % (l, i)) for i in range(KC)] for l in range(L)]
    rwinb_b = [[Buf("rwin%d_%d" % (r, i)) for i in range(64)] for r in range(cfg.n_rec)]
    rwoutb_b = [[Buf("rwout%d_%d" % (r, i)) for i in range(KC)] for r in range(cfg.n_rec)]

    cast_jobs = [[] for _ in range(L)]
    cur_l = [0]

    class _CastQ:
        def dma(self, q, out, in_, w):
            cast_jobs[cur_l[0]].append((out, in_, w))
    PQ = _CastQ()

    def cast_ffn(l):
        for i in range(FC):
            PQ.dma("pool", wu_b[l][i], wu_f[l][i], w=[wub_b[l][i]])
            PQ.dma("pool", wg_b[l][i], wg_f[l][i], w=[wgb_b[l][i]])
        for i in range(KC):
            PQ.dma("pool", wd_b[l][i], wd_f[l][i], w=[wdb_b[l][i]])

    def cast_rec(r):
        for hd in range(16):
            for which in range(4):
                i = which * 16 + hd
                PQ.dma("pool", rwin_b[r][i], rwin_f[r][i], w=[rwinb_b[r][i]])
        for i in range(KC):
            PQ.dma("pool", rwout_b[r][i], rwout_f[r][i], w=[rwoutb_b[r][i]])

    awinb_b = [[Buf("awin%d_%d" % (a, i)) for i in range(18)] for a in range(cfg.n_att)]
    awkrb_b = [Buf("awkr%d" % a) for a in range(cfg.n_att)]
    awkrrotb_b = [Buf("awkrrot%d" % a) for a in range(cfg.n_att)]
    wuqb_b = [[Buf("wuq%d_%d" % (a, i)) for i in range(4)] for a in range(cfg.n_att)]
    wrotb_b = [Buf("wrot%d" % a) for a in range(cfg.n_att)]
    wukvkb_b = [Buf("wukvk%d" % a) for a in range(cfg.n_att)]
    wukvvb_b = [Buf("wukvv%d" % a) for a in range(cfg.n_att)]
    awoutb_b = [[Buf("awout%d_%d" % (a, i)) for i in range(KC)] for a in range(cfg.n_att)]

    def cast_att(a):
        for i in range(18):
            PQ.dma("pool", awin_b[a][i], awin_f[a][i], w=[awinb_b[a][i]])
        PQ.dma("pool", awkr_b[a], awkr_f[a], w=[awkrb_b[a]])
        for i in range(4):
            PQ.dma("pool", wuq_b[a][i], wuq_f[a][i], w=[wuqb_b[a][i]])
        PQ.dma("pool", wukvk_b[a], wukvk_f[a], w=[wukvkb_b[a]])
        PQ.dma("pool", wukvv_b[a], wukvv_f[a], w=[wukvvb_b[a]])
        for i in range(KC):
            PQ.dma("pool", awout_b[a][i], awout_f[a][i], w=[awoutb_b[a][i]])

    ri = 0
    ai = 0
    for l in range(L):
        cur_l[0] = l
        if cfg.layers[l] == "r":
            cast_rec(ri); ri += 1
        if cfg.layers[l] == "a":
            cast_att(ai); ai += 1
        cast_ffn(l)
    PACE = sb("PACE", [128, 1])

    def emit_casts(l, lo, hi, pace=None):
        for (out, in_, w) in cast_jobs[l][lo:hi]:
            P.dma("pool", out, in_, r=([pace] if pace is not None else []), w=w)

    emit_casts(0, 0, len(cast_jobs[0]))
    if not PACE_ON:
        for l_ in range(1, L):
            emit_casts(l_, 0, len(cast_jobs[l_]))

    wa_ctr = [0]

    mix_ctr = [0]

    def next_wa():
        i = wa_ctr[0] % NWA
        wa_ctr[0] += 1
        return WA[i], WAb[i]

    def next_mw():
        i = mix_ctr[0] % 8
        mix_ctr[0] += 1
        if i < 4:
            return WA[i], WAb[i]
        return WDh[i - 4]

    def rms_rstd(w, nfeat_inv, src_chunks, eps_ap, eps_b):
        n = len(src_chunks)
        for i, (ap, bufs) in enumerate(src_chunks):
            sq, sqb = SQ[i % 2], SQb[i % 2]
            ACTF(sq[:, :w], ap, AF.Square, bufs, [sqb])
            MM(PS[6][:, :w], ONES[:, :], sq[:, :w], i == 0, i == n - 1, [sqb, ONESb], [PSb[6]])
        ACTF(TMPN[:, :w], PS[6][:, :w], AF.Sqrt, [PSb[6], eps_b], [TMPNb], bias=eps_ap, scale=nfeat_inv)
        VRECIP(RSTD[:, :w], TMPN[:, :w], [TMPNb], [RSTDb])

    def norm_H(gcol, w):
        rms_rstd(w, 1.0 / D, [(X[:, kc, :w], [Xb[kc]]) for kc in range(KC)], EPSC[:, 0:1], EPSCb)
        for kc in range(KC):
            VSTT(H[:, kc, :w], X[:, kc, :w], GN[:, gcol + kc:gcol + kc + 1], RSTD[:, :w], ALU.mult, ALU.mult,
                 [Xb[kc], GNb, RSTDb], [Hb[kc]])

    def ffn_tile(l, w, after_chunk=None):
        norm_H((L + l) * KC, w)
        for fc in range(FC):
            wu, wub = next_wa()
            wg, wgb = next_wa()
            P.dma("sp", wu[:, :], wu_b[l][fc], r=[wub_b[l][fc]], w=[wub])
            P.dma("sp", wg[:, :], wg_b[l][fc], r=[wgb_b[l][fc]], w=[wgb])
            pu, pub = PS[fc % 2], PSb[fc % 2]
            pa, pab = PS[2 + fc % 2], PSb[2 + fc % 2]
            for kc in range(KC):
                MM(pu[:, :w], wu[:, kc * 128:(kc + 1) * 128], H[:, kc, :w], kc == 0, kc == KC - 1, [wub, Hb[kc]], [pub])
            for kc in range(KC):
                MM(pa[:, :w], wg[:, kc * 128:(kc + 1) * 128], H[:, kc, :w], kc == 0, kc == KC - 1, [wgb, Hb[kc]], [pab])
            asb, asbb = ASB[fc % 2], ASBb[fc % 2]
            acc, accb = ACC[fc % 2], ACCb[fc % 2]
            sl, slb = acc, accb
            cb = (l * FC + fc) * 4
            ACOPY(asb[:, 0:2], CC[:, l, fc, :], [CCb[l]], [asbb])
            ACOPY(asb[:, 2:2 + w], pa[:, :w], [pab], [asbb])
            ACTF(acc[:, :w], pa[:, :w], AF.Identity, [pab, CWb], [accb], scale=CW[:, cb + 2:cb + 3],
                 bias=CW[:, cb + 3:cb + 4])
            VSTT(acc[:, :w], asb[:, 1:1 + w], CW[:, cb + 1:cb + 2], acc[:, :w], ALU.mult, ALU.add,
                 [asbb, accb, CWb], [accb])
            VSTT(acc[:, :w], asb[:, 0:w], CW[:, cb:cb + 1], acc[:, :w], ALU.mult, ALU.add,
                 [asbb, accb, CWb], [accb])
            ACOPY(CC[:, l, fc, :], asb[:, w:w + 2], [asbb], [CCb[l]])
            ACTF(sl[:, :w], acc[:, :w], AF.Silu, [accb], [slb])
            VTT(Gv(fc, w), sl[:, :w], pu[:, :w], ALU.mult, [slb, pub], [Gb[fc]])
        P.dma("sp", WD[0][:, :], wd_b[l][0], r=[wdb_b[l][0]], w=[WDb[0]])
        for j in range(KC):
            wd, wdb = WD[j % 2], WDb[j % 2]
            if j + 1 < KC:
                P.dma("sp", WD[(j + 1) % 2][:, :], wd_b[l][j + 1], r=[wdb_b[l][j + 1]], w=[WDb[(j + 1) % 2]])
            py, pyb = PS[4 + j % 2], PSb[4 + j % 2]
            for fc in range(FC):
                MM(py[:, :w], wd[:, fc * 128:(fc + 1) * 128], Gv(fc, w), fc == 0, fc == FC - 1, [wdb, Gb[fc]], [pyb])
            VTT(X[:, j, :w], X[:, j, :w], py[:, :w], ALU.add, [Xb[j], pyb], [Xb[j]])
            if after_chunk is not None:
                after_chunk(j)

    NG = 4
    MA_BYTES = 48 * 1024
    MA = sb("MA", [128, MA_BYTES // 2], BF16)
    marena = []

    class Bump:
        def __init__(self, t, arena, base=0, limit=None):
            self.t, self.arena, self.off, self.limit = t, arena, base, limit

        def get(self, name, nbytes, dt=BF16, nbuf=None):
            assert nbytes % 4 == 0
            if self.limit is not None:
                assert self.off + nbytes <= self.limit, (name, self.off, nbytes, self.limit)
            ap = self.t[:, self.off // 2:(self.off + nbytes) // 2]
            if dt == F32:
                ap = ap.bitcast(F32)
            b = Buf(name, self.arena, self.off, self.off + nbytes)
            self.off += nbytes
            return ap, b

    if cfg.n_rec > 0:
        mr = Bump(MA, marena, 0, MA_BYTES)
        S32f, _ = mr.get("S32", 16 * 128 * 4, F32); marena.pop()
        S32 = S32f.rearrange("p (h v) -> p h v", v=128)
        S32b = [Buf("S32_%d" % i, marena, i * 512, (i + 1) * 512) for i in range(16)]
        SBFf, _ = mr.get("SBF", 16 * 128 * 2); marena.pop()
        SBF = SBFf.rearrange("p (h v) -> p h v", v=128)
        SBFb = [Buf("SBF_%d" % i, marena, 8192 + i * 256, 8192 + (i + 1) * 256) for i in range(16)]

        def mk(name, n, nbytes, dt=BF16):
            aps, bufs = [], []
            for i in range(n):
                a_, b_ = mr.get("%s%d" % (name, i), nbytes, dt)
                aps.append(a_); bufs.append(b_)
            return aps, bufs
        QT, QTb = mk("QT", NG, TT * 2)
        KT, KTb = mk("KT", NG, TT * 2)
        KX, KXb = mk("KX", NG, TT * 2)
        QI, QIb = mk("QI", NG, TT * 2)
        KU, KUb = mk("KU", NG, TT * 2)
        VB, VBb = mk("VB", NG, TT * 2)
        EBE, EBEb = mk("EBE", NG, 16, F32)
        MT, MTb = mk("MT", 3, TT * 4, F32)
        PTl, PTbl = mk("PT", 1, TT * 4, F32); PT, PTb = PTl[0], PTbl[0]
        AT, ATb = mk("AT", 2, 256)
        KUT, KUTb = mk("KUT", 2, 256)
        O32 = [ga_f32(16384 + s * 2048, TT) for s in range(NG)]
        O32b = [Buf("O32_%d" % s, garena, 16384 + s * 2048, 16384 + (s + 1) * 2048) for s in range(NG)]
        SG = [ga_f32(24576 + s * 2048, TT) for s in range(NG)]
        SGb = [Buf("SG_%d" % s, garena, 24576 + s * 2048, 24576 + (s + 1) * 2048) for s in range(NG)]
        TG = [ga_f32(32768 + s * 2048, TT) for s in range(6)]
        TGb = [Buf("TG_%d" % s, garena, 32768 + s * 2048, 32768 + (s + 1) * 2048) for s in range(6)]
        P.op("dve", lambda e: e.memset(EPSR[:, :], EPS), w=[EPSRb])
    slot_ctr = [0]

    def rec_tile(l, r, ti, w):
        if ti == 0:
            VMEMSET(S32[:, :, :], 0.0, S32b)
            VMEMSET(SBF[:, :, :], 0.0, SBFb)
        bw = 16 if w == NMETA else 128
        nblk = w // bw
        chunks = [(0, 16)] if w == NMETA else [(c * 64, 64) for c in range(w // 64)]
        blocks = [(b * bw, bw) for b in range(nblk)]
        norm_H(l * KC, w)
        T1, T2, T3, T4, T5, T6 = TG
        T1b, T2b, T3b, T4b, T5b, T6b = TGb
        M1, M2, M3 = MT
        M1b, M2b, M3b = MTb
        for g0 in range(0, 16, NG):
            for s in range(NG):
                hd = g0 + s
                lbc = LB[:, r * 16 + hd:r * 16 + hd + 1]
                omc = OML[:, r * 16 + hd:r * 16 + hd + 1]
                tiles = []
                for which in range(4):
                    wt, wtb = next_mw()
                    i = which * 16 + hd
                    P.dma("sp", wt[:, :], rwin_b[r][i], r=[rwinb_b[r][i]], w=[wtb])
                    tiles.append((wt, wtb))
                for which, bank in ((0, 0), (1, 1), (3, 2)):
                    wt, wtb = tiles[which]
                    for kc in range(KC):
                        MM(PS[bank][:, :w], wt[:, kc * 128:(kc + 1) * 128], H[:, kc, :w], kc == 0, kc == KC - 1,
                           [wtb, Hb[kc]], [PSb[bank]])
                wt, wtb = tiles[2]
                for bi, (c0, _) in enumerate(blocks):
                    for kc in range(KC):
                        MM(PS[3][:bw, bi * 128:(bi + 1) * 128], H[:, kc, c0:c0 + bw], wt[:, kc * 128:(kc + 1) * 128],
                           kc == 0, kc == KC - 1, [wtb, Hb[kc]], [PSb[3]])
                ACOPY(VB[s][:bw, :nblk * 128], PS[3][:bw, :nblk * 128], [PSb[3]], [VBb[s]])
                ACTF(T1[:, :w], PS[1][:, :w], AF.Sigmoid, [PSb[1]], [T1b])
                ACTF(M2[:, :w], PS[0][:, :w], AF.Silu, [PSb[0]], [M2b])
                ACTF(SG[s][:, :w], PS[2][:, :w], AF.Silu, [PSb[2]], [SGb[s]])
                ACTF(T1[:, :w], T1[:, :w], AF.Identity, [T1b, OMLb, LBb], [T1b], scale=omc, bias=lbc)
                ACTF(T2[:, :w], T1[:, :w], AF.Ln, [T1b], [T2b])
                ACTF(T3[:, :w], T1[:, :w], AF.Identity, [T1b, ONECb], [T3b], scale=-1.0, bias=ONEC[:, 0:1])
                P.op("dve", lambda e: e.tensor_tensor_scan(out=T4[:, :w], data0=RESET64[:, :w], data1=T2[:, :w],
                                                           initial=0.0, op0=ALU.mult, op1=ALU.add),
                     r=[CONSb, T2b], w=[T4b])
                P.op("dve", lambda e: e.tensor_tensor_scan(out=T5[:, :w], data0=RESET128[:, :w], data1=T2[:, :w],
                                                           initial=0.0, op0=ALU.mult, op1=ALU.add),
                     r=[CONSb, T2b], w=[T5b])
                ACTF(M1[:, :w], T4[:, :w], AF.Exp, [T4b], [M1b])
                VTT(QT[s][:, :w], M2[:, :w], M1[:, :w], ALU.mult, [M1b, M2b], [QTb[s]])
                ACTF(M3[:, :w], T4[:, :w], AF.Exp, [T4b], [M3b], scale=-1.0)
                VTT(KT[s][:, :w], T3[:, :w], M3[:, :w], ALU.mult, [T3b, M3b], [KTb[s]])
                for (cs, cw) in chunks:
                    ACTF(M1[:, cs:cs + cw], T4[:, cs:cs + cw], AF.Exp, [T4b], [M1b], scale=-1.0,
                         bias=T4[:, cs + cw - 1:cs + cw])
                VTT(KX[s][:, :w], T3[:, :w], M1[:, :w], ALU.mult, [T3b, M1b], [KXb[s]])
                ACTF(M3[:, :w], T5[:, :w], AF.Exp, [T5b], [M3b])
                VTT(QI[s][:, :w], M2[:, :w], M3[:, :w], ALU.mult, [M2b, M3b], [QIb[s]])
                for bi, (c0, _) in enumerate(blocks):
                    ACOPY(EBE[s][:, bi:bi + 1], M3[:, c0 + bw - 1:c0 + bw], [M3b], [EBEb[s]])
                for (c0, _) in blocks:
                    ACTF(M1[:, c0:c0 + bw], T5[:, c0:c0 + bw], AF.Exp, [T5b], [M1b], scale=-1.0,
                         bias=T5[:, c0 + bw - 1:c0 + bw])
                VTT(KU[s][:, :w], T3[:, :w], M1[:, :w], ALU.mult, [T3b, M1b], [KUb[s]])
            bsteps = [(bi, c0, s) for bi, (c0, _) in enumerate(blocks) for s in range(NG)]
            bsteps.sort(key=lambda t_: (t_[0] + (1 if t_[2] >= 2 else 0), t_[2] >= 2, t_[0], t_[2]))
            kbase = slot_ctr[0]
            slot_ctr[0] += len(bsteps)

            def rb_front(i):
                bi, c0, s = bsteps[i]
                k = kbase + i
                q4 = (k % 4) * 128
                at, atb = AT[k % 2], ATb[k % 2]
                kut, kutb = KUT[k % 2], KUTb[k % 2]
                sbank = (3, 0)[k % 2]
                st, stb = PS[sbank][:, q4:q4 + 128], PSb[sbank]
                MM(st[:bw, :bw], KT[s][:, c0:c0 + bw], QT[s][:, c0:c0 + bw], True, True, [KTb[s], QTb[s]], [stb])
                if bw == 128:
                    MM(st[0:64, 64:128], KX[s][:, c0:c0 + 64], QT[s][:, c0 + 64:c0 + 128], True, True,
                       [KXb[s], QTb[s]], [stb])
                VTT(at[:bw, :bw], st[:bw, :bw], MASKBD[:bw, :bw], ALU.mult, [stb, CONSb], [atb])
                pst = PST[:, (k % 8) * 128:(k % 8) * 128 + 128]
                P.op("pe", lambda e: e.transpose(out=pst[:bw, :], in_=KU[s][:, c0:c0 + bw], identity=IDENT[:, :]),
                     r=[KUb[s], IDENTb], w=[PSTb])
                ACOPY(kut[:bw, :], pst[:bw, :], [PSTb], [kutb])

            def rb_back(i):
                bi, c0, s = bsteps[i]
                hd = g0 + s
                k = kbase + i
                q4 = (k % 4) * 128
                at, atb = AT[k % 2], ATb[k % 2]
                kut, kutb = KUT[k % 2], KUTb[k % 2]
                obank = (4, 1)[k % 2]
                osl, oslb = PS[obank][:, q4:q4 + 128], PSb[obank]
                MM(osl[:, :bw], VB[s][:bw, bi * 128:(bi + 1) * 128], at[:bw, :bw], True, False, [VBb[s], atb], [oslb])
                MM(osl[:, :bw], SBF[:, hd, :], QI[s][:, c0:c0 + bw], False, True, [SBFb[hd], QIb[s]], [oslb])
                ubank = (5, 2)[k % 2]
                usl, uslb = PS[ubank][:, q4:q4 + 128], PSb[ubank]
                MM(usl[:, :], kut[:bw, :], VB[s][:bw, bi * 128:(bi + 1) * 128], True, True, [kutb, VBb[s]], [uslb])
                VSTT(S32[:, hd, :], S32[:, hd, :], EBE[s][:, bi:bi + 1], usl[:, :], ALU.mult, ALU.add,
                     [S32b[hd], EBEb[s], uslb], [S32b[hd]])
                ACOPY(SBF[:, hd, :], S32[:, hd, :], [S32b[hd]], [SBFb[hd]])
                ACOPY(O32[s][:, c0:c0 + bw], osl[:, :bw], [oslb], [O32b[s]])

            for i in range(len(bsteps)):
                if PIPE_R:
                    if i == 0:
                        rb_front(0)
                    if i + 1 < len(bsteps):
                        rb_front(i + 1)
                else:
                    rb_front(i)
                rb_back(i)
            for s in range(NG):
                hd = g0 + s
                rms_rstd(w, 1.0 / 128, [(O32[s][:, :w], [O32b[s]])], EPSR[:, 0:1], EPSRb)
                VSTT(PT[:, :w], O32[s][:, :w], ONG[:, r:r + 1], RSTD[:, :w], ALU.mult, ALU.mult,
                     [O32b[s], ONGb, RSTDb], [PTb])
                VTT(Gv(hd, w), PT[:, :w], SG[s][:, :w], ALU.mult, [PTb, SGb[s]], [Gb[hd]])
        for j in range(KC):
            wt, wtb = next_mw()
            P.dma("sp", wt[:, :], rwout_b[r][j], r=[rwoutb_b[r][j]], w=[wtb])
            py, pyb = PS[j % 2], PSb[j % 2]
            for kc in range(KC):
                MM(py[:, :w], wt[:, kc * 128:(kc + 1) * 128], Gv(kc, w), kc == 0, kc == KC - 1, [wtb, Gb[kc]], [pyb])
            VTT(X[:, j, :w], X[:, j, :w], py[:, :w], ALU.add, [Xb[j], pyb], [Xb[j]])


    if cfg.n_att > 0:
        ntile = len(cfg.tiles)
        mat = Bump(MA, marena, 0, MA_BYTES)
        QACf, _ = mat.get("QAC", 16384); marena.pop()
        QAC = QACf.rearrange("p (h t) -> p h t", t=TT)
        QACb = [Buf("QAC%d" % h, marena, h * 1024, (h + 1) * 1024) for h in range(16)]
        KAf, KAb = mat.get("KA", 5120); KA = KAf.rearrange("p (j t) -> p j t", t=640)
        VAf, VAb = mat.get("VA", 2560); VA = VAf.rearrange("p (b v) -> p b v", v=256)
        CQNf, CQNb = mat.get("CQN", 4096); CQN = CQNf.rearrange("p (c t) -> p c t", t=TT)
        CKVNf, CKVNb = mat.get("CKVN", 2048); CKVN = CKVNf.rearrange("p (c t) -> p c t", t=TT)
        KL, KLb, VL, VLb = [], [], [], []
        for i in range(2):
            a_, b_ = mat.get("KL%d" % i, 4096); KL.append(a_.rearrange("p (e t) -> p e t", e=2)); KLb.append(b_)
        for i in range(2):
            a_, b_ = mat.get("VL%d" % i, 2048); VL.append(a_.rearrange("p (b v) -> p b v", v=128)); VLb.append(b_)
        PTS, PTSb, PTM, PTMb = [], [], [], []
        for i in range(2):
            a_, b_ = mat.get("PTS%d" % i, 1024); PTS.append(a_); PTSb.append(b_)
        for i in range(3):
            a_, b_ = mat.get("PTM%d" % i, 1024); PTM.append(a_); PTMb.append(b_)
        KROPE, KROPEb = mat.get("KROPE", 1024)
        gat = Bump(GA, garena, 16384, 45056)
        KCS, KCSb, VMS, VMSb = [], [], [], []
        for i in range(2):
            a_, b_ = gat.get("KCS%d" % i, 1024); KCS.append(a_); KCSb.append(b_)
        for i in range(2):
            a_, b_ = gat.get("VMS%d" % i, 2048); VMS.append(a_); VMSb.append(b_)
        ROPEC, ROPECb = gat.get("ROPEC", 2048, F32)
        ROPES, ROPESb = gat.get("ROPES", 2048, F32)
        TMPA, TMPAb = gat.get("TMPA", 2048, F32)
        TMPB, TMPBb = gat.get("TMPB", 2048, F32)
        TD, TDb = gat.get("TD", 2048, F32)
        WX0, WX0b = gat.get("WX0", 4096)
        ESINK = sb("ESINK", [128, NA * 8]); ESINKb = Buf("ESINK")
        ANORM = sb("ANORM", [128, NA * 6]); ANORMb = Buf("ANORM")
        EPSQ = sb("EPSQ", [128, 1]); EPSQb = Buf("EPSQ")
        P.dma("sp", ESINK[:, :], sinkc_d[:, :], w=[ESINKb])
        P.dma("sp", ANORM[:, :], anorm_d[:, :], w=[ANORMb])
        ACTF(ESINK[:, :], ESINK[:, :], AF.Exp, [ESINKb], [ESINKb])
        VMEMSET(EPSQ[:, :], EPS, [EPSQb])
        kcatb = [[[Buf("kcat%d_%d_%d" % (a, t, h)) for h in range(16)] for t in range(ntile)] for a in range(cfg.n_att)]
        vmlb = [[[Buf("vml%d_%d_%d" % (a, t, b)) for b in range(4)] for t in range(ntile)] for a in range(cfg.n_att)]
    actr = [0, 0, 0]

    def att_prep(a):
        rotb, rotbb = WX0, WX0b
        rv2 = rotb.rearrange("p (q m) -> p q m", m=32)
        for gq in range(4):
            wt, wtb = next_mw()
            P.dma("sp", wt[:, 0:1536], wuq_b[a][gq], r=[wuqb_b[a][gq]], w=[wtb])
            wv2 = wt[:, 0:1536].rearrange("p (q m) -> p q m", m=96)
            VTS(rv2[:, gq * 16:(gq + 1) * 16, 0:16], wv2[:, :, 80:96], -1.0, None, ALU.mult, None, [wtb], [rotbb])
            VCOPY(rv2[:, gq * 16:(gq + 1) * 16, 16:32], wv2[:, :, 64:80], [wtb], [rotbb])
        P.dma("sp", wrot_b[a], rotb[:, :], r=[rotbb], w=[wrotb_b[a]])
        wt, wtb = next_mw()
        P.dma("sp", wt[:, 0:512], awkr_b[a], r=[awkrb_b[a]], w=[wtb])
        wr, wrb = next_mw()
        wv3 = wt[:, 0:512].rearrange("p (q m) -> p q m", m=32)
        rv3 = wr[:, 0:512].rearrange("p (q m) -> p q m", m=32)
        VTS(rv3[:, :, 0:16], wv3[:, :, 16:32], -1.0, None, ALU.mult, None, [wtb], [wrb])
        VCOPY(rv3[:, :, 16:32], wv3[:, :, 0:16], [wtb], [wrb])
        P.dma("sp", awkrrot_b[a], wr[:, 0:512], r=[wrb], w=[awkrrotb_b[a]])

    def att_tile(l, a, ti, t0, w):
        bw = 16 if w == NMETA else 128
        nblk = w // bw
        blocks = [(b * bw, bw) for b in range(nblk)]
        pw0 = 0 if ti == 0 else (16 if ti == 1 else 128)
        norm_H(l * KC, w)
        for c in range(8):
            wt, wtb = next_mw()
            P.dma("sp", wt[:, :], awin_b[a][c], r=[awinb_b[a][c]], w=[wtb])
            for e in range(2):
                h = 2 * c + e
                ps, psb = PS[h % 2], PSb[h % 2]
                for kc in range(KC):
                    MM(ps[0:64, :w], wt[:, kc * 128 + e * 64:kc * 128 + e * 64 + 64], H[:, kc, :w], kc == 0,
                       kc == KC - 1, [wtb, Hb[kc]], [psb])
                ACOPY(QAC[0:64, h, :w], ps[0:64, :w], [psb], [QACb[h]])
        for c in range(2):
            wt, wtb = next_mw()
            P.dma("sp", wt[:, :], awin_b[a][8 + c], r=[awinb_b[a][8 + c]], w=[wtb])
            for e in range(2):
                j = 2 * c + e
                ps, psb = PS[j % 2], PSb[j % 2]
                for kc in range(KC):
                    MM(ps[0:64, :w], wt[:, kc * 128 + e * 64:kc * 128 + e * 64 + 64], H[:, kc, :w], kc == 0,
                       kc == KC - 1, [wtb, Hb[kc]], [psb])
                ACOPY(KA[0:64, j, 128:128 + w], ps[0:64, :w], [psb], [KAb])
        wvs = []
        for g in range(2):
            wt, wtb = next_mw()
            P.dma("sp", wt[:, :], awin_b[a][10 + g], r=[awinb_b[a][10 + g]], w=[wtb])
            wvs.append((wt, wtb))
        for bi, (c0, _) in enumerate(blocks):
            bank = 2 + bi // 2
            off = (bi % 2) * 256
            for g, (wt, wtb) in enumerate(wvs):
                for kc in range(KC):
                    MM(PS[bank][:bw, off + g * 128:off + (g + 1) * 128], H[:, kc, c0:c0 + bw],
                       wt[:, kc * 128:(kc + 1) * 128], kc == 0, kc == KC - 1, [wtb, Hb[kc]], [PSb[bank]])
            ACOPY(VA[:bw, 1 + bi, :], PS[bank][:bw, off:off + 256], [PSb[bank]], [VAb])
        def v3(ap, rows, half):
            return ap[:rows, half * 256:(half + 1) * 256].rearrange("p (e q) -> p e q", e=2)[:, :, :bw]

        def m3(mask, rows):
            return mask[:rows, :].rearrange("p (e q) -> p e q", e=2)[:, :, :bw]

        sw = [(c, bi) for c in range(8) for bi in range(nblk)]
        swbase = actr[0]
        actr[0] += len(sw)

        def sw_geo(bi):
            c0 = blocks[bi][0]
            if bi == 0:
                return c0, pw0, 0, 0, 128 + c0, 1 + bi
            return c0, 128, 128 + c0 - 128, bi, 128 + c0, 1 + bi

        def swa_front(i):
            c, bi = sw[i]
            kv = c // 2
            k = swbase + i
            ps, psb = PS[k % 2], PSb[k % 2]
            pt, ptb = PTS[k % 2], PTSb[k % 2]
            c0, pw, pc0, pslot, cc0, cslot = sw_geo(bi)
            for e in range(2):
                h = 2 * c + e
                if pw > 0:
                    MM(ps[:pw, e * 128:e * 128 + bw], KA[0:64, kv, pc0:pc0 + pw], QAC[0:64, h, c0:c0 + bw],
                       True, True, [KAb, QACb[h]], [psb])
                MM(ps[:bw, 256 + e * 128:256 + e * 128 + bw], KA[0:64, kv, cc0:cc0 + bw],
                   QAC[0:64, h, c0:c0 + bw], True, True, [KAb, QACb[h]], [psb])
            if pw > 0:
                ACTF(v3(pt, pw, 0), v3(ps, pw, 0), AF.Exp, [psb], [ptb], scale=0.125)
                mk_ = PREV16_2 if pw == 16 else STRICT2
                VTT(v3(pt, pw, 0), v3(pt, pw, 0), m3(mk_, pw), ALU.mult, [ptb, CONSb], [ptb])
            ACTF(v3(pt, bw, 1), v3(ps, bw, 1), AF.Exp, [psb], [ptb], scale=0.125)
            VTT(v3(pt, bw, 1), v3(pt, bw, 1), m3(TRI2, bw), ALU.mult, [ptb, CONSb], [ptb])

        def swa_back(i):
            c, bi = sw[i]
            kv = c // 2
            k = swbase + i
            pt, ptb = PTS[k % 2], PTSb[k % 2]
            po, pob = PS[2 + 2 * (c % 2)], PSb[2 + 2 * (c % 2)]
            pd, pdb = PS[3 + 2 * (c % 2)], PSb[3 + 2 * (c % 2)]
            c0, pw, pc0, pslot, cc0, cslot = sw_geo(bi)
            for e in range(2):
                rs = slice(e * 64, (e + 1) * 64)
                if pw > 0:
                    MM(po[rs, c0:c0 + bw], VA[:pw, pslot, kv * 64:(kv + 1) * 64], pt[:pw, e * 128:e * 128 + bw],
                       True, False, [VAb, ptb], [pob])
                    MM(pd[rs, c0:c0 + bw], ONES[:pw, 0:64], pt[:pw, e * 128:e * 128 + bw],
                       True, False, [ONESb, ptb], [pdb])
                MM(po[rs, c0:c0 + bw], VA[:bw, cslot, kv * 64:(kv + 1) * 64],
                   pt[:bw, 256 + e * 128:256 + e * 128 + bw], pw == 0, True, [VAb, ptb], [pob])
                MM(pd[rs, c0:c0 + bw], ONES[:bw, 0:64], pt[:bw, 256 + e * 128:256 + e * 128 + bw],
                   pw == 0, True, [ONESb, ptb], [pdb])
            if bi == nblk - 1:
                VTS(TD[:, :w], pd[:, :w], ESINK[:, a * 8 + c:a * 8 + c + 1], None, ALU.add, None, [pdb, ESINKb], [TDb])
                VRECIP(TD[:, :w], TD[:, :w], [TDb], [TDb])
                VTT(Gv(c, w), po[:, :w], TD[:, :w], ALU.mult, [pob, TDb], [Gb[c]])

        for i in range(len(sw)):
            if PIPE_S:
                if i == 0:
                    swa_front(0)
                if i + 1 < len(sw):
                    swa_front(i + 1)
            else:
                swa_front(i)
            swa_back(i)
        cwid = min(w, 128)
        ACOPY(KA[0:64, :, 0:cwid], KA[0:64, :, 128 + w - cwid:128 + w], [KAb], [KAb])
        ACOPY(VA[:cwid, 0, :], VA[:cwid, nblk, :], [VAb], [VAb])
        for c in range(2):
            wt, wtb = next_mw()
            P.dma("sp", wt[:, :], awin_b[a][16 + c], r=[awinb_b[a][16 + c]], w=[wtb])
            for kc in range(KC):
                MM(PS[4 + c][:, :w], wt[:, kc * 128:(kc + 1) * 128], H[:, kc, :w], kc == 0, kc == KC - 1,
                   [wtb, Hb[kc]], [PSb[4 + c]])
        rms_rstd(w, 1.0 / 256, [(PS[4 + c][:, :w], [PSb[4 + c]]) for c in range(2)], EPSQ[:, 0:1], EPSQb)
        for c in range(2):
            VSTT(CKVN[:, c, :w], PS[4 + c][:, :w], ANORM[:, a * 6 + 4 + c:a * 6 + 5 + c], RSTD[:, :w], ALU.mult,
                 ALU.mult, [PSb[4 + c], ANORMb, RSTDb], [CKVNb])
        for c in range(4):
            wt, wtb = next_mw()
            P.dma("sp", wt[:, :], awin_b[a][12 + c], r=[awinb_b[a][12 + c]], w=[wtb])
            for kc in range(KC):
                MM(PS[c][:, :w], wt[:, kc * 128:(kc + 1) * 128], H[:, kc, :w], kc == 0, kc == KC - 1,
                   [wtb, Hb[kc]], [PSb[c]])
        rms_rstd(w, 1.0 / 512, [(PS[c][:, :w], [PSb[c]]) for c in range(4)], EPSQ[:, 0:1], EPSQb)
        for c in range(4):
            VSTT(CQN[:, c, :w], PS[c][:, :w], ANORM[:, a * 6 + c:a * 6 + c + 1], RSTD[:, :w], ALU.mult,
                 ALU.mult, [PSb[c], ANORMb, RSTDb], [CQNb])
        P.dma("sp", ROPEC[64:96, :w], ropec_d[:, t0:t0 + w], w=[ROPECb])
        P.dma("sp", ROPES[64:96, :w], ropes_d[:, t0:t0 + w], w=[ROPESb])
        wkr, wkrb = next_mw()
        P.dma("sp", wkr[:, 0:512], awkr_b[a], r=[awkrb_b[a]], w=[wkrb])
        wkrr, wkrrb = next_mw()
        P.dma("sp", wkrr[:, 0:512], awkrrot_b[a], r=[awkrrotb_b[a]], w=[wkrrb])
        for kc in range(KC):
            MM(PS[4][64:96, :w], wkr[:, kc * 32:(kc + 1) * 32], H[:, kc, :w], kc == 0, kc == KC - 1,
               [wkrb, Hb[kc]], [PSb[4]])
        for kc in range(KC):
            MM(PS[5][64:96, :w], wkrr[:, kc * 32:(kc + 1) * 32], H[:, kc, :w], kc == 0, kc == KC - 1,
               [wkrrb, Hb[kc]], [PSb[5]])
        VTT(TMPA[64:96, :w], PS[4][64:96, :w], ROPEC[64:96, :w], ALU.mult, [PSb[4], ROPECb], [TMPAb])
        VTT(TMPB[64:96, :w], PS[5][64:96, :w], ROPES[64:96, :w], ALU.mult, [PSb[5], ROPESb], [TMPBb])
        VTT(KROPE[64:96, :w], TMPA[64:96, :w], TMPB[64:96, :w], ALU.add, [TMPAb, TMPBb], [KROPEb])
        P.dma("sp", WX0[:, :], wrot_b[a], r=[wrotb_b[a]], w=[WX0b])
        for gq in range(4):
            wt, wtb = next_mw()
            P.dma("sp", wt[:, 0:1536], wuq_b[a][gq], r=[wuqb_b[a][gq]], w=[wtb])
            for hh in range(4):
                h = gq * 4 + hh
                pm, pmb = PS[2 * (h % 2)], PSb[2 * (h % 2)]
                pr, prb = PS[1 + 2 * (h % 2)], PSb[1 + 2 * (h % 2)]
                for kc in range(4):
                    MM(pm[0:96, :w], wt[:, hh * 384 + kc * 96:hh * 384 + (kc + 1) * 96], CQN[:, kc, :w],
                       kc == 0, kc == 3, [wtb, CQNb], [pmb])
                for kc in range(4):
                    MM(pr[64:96, :w], WX0[:, h * 128 + kc * 32:h * 128 + (kc + 1) * 32], CQN[:, kc, :w],
                       kc == 0, kc == 3, [WX0b, CQNb], [prb])
                ACOPY(QAC[0:64, h, :w], pm[0:64, :w], [pmb], [QACb[h]])
                VTT(TMPA[64:96, :w], pm[64:96, :w], ROPEC[64:96, :w], ALU.mult, [pmb, ROPECb], [TMPAb])
                VTT(TMPB[64:96, :w], pr[64:96, :w], ROPES[64:96, :w], ALU.mult, [prb, ROPESb], [TMPBb])
                VTT(QAC[64:96, h, :w], TMPA[64:96, :w], TMPB[64:96, :w], ALU.add, [TMPAb, TMPBb], [QACb[h]])
        wk, wkb = next_mw()
        P.dma("sp", wk[:, :], wukvk_b[a], r=[wukvkb_b[a]], w=[wkb])
        for h in range(16):
            c, e = h // 2, h % 2
            ps, psb = PS[4 + h % 2], PSb[4 + h % 2]
            for kc in range(2):
                MM(ps[0:64, :w], wk[:, c * 256 + kc * 128 + e * 64:c * 256 + kc * 128 + e * 64 + 64],
                   CKVN[:, kc, :w], kc == 0, kc == 1, [wkb, CKVNb], [psb])
            kcs, kcsb = KCS[h % 2], KCSb[h % 2]
            ACOPY(kcs[0:64, :w], ps[0:64, :w], [psb], [kcsb])
            ACOPY(kcs[64:96, :w], KROPE[64:96, :w], [KROPEb], [kcsb])
            P.dma("sp", kcat_s[a][h, :, t0:t0 + w], kcs[0:96, :w], r=[kcsb], w=[kcatb[a][ti][h]])
        wv, wvb = next_mw()
        P.dma("sp", wv[:, :], wukvv_b[a], r=[wukvvb_b[a]], w=[wvb])
        for bi, (c0, _) in enumerate(blocks):
            vms, vmsb = VMS[bi % 2], VMSb[bi % 2]
            for j in range(2):
                ps, psb = PS[2 * (bi % 2) + j], PSb[2 * (bi % 2) + j]
                for kc in range(2):
                    MM(ps[:bw, :], CKVN[:, kc, c0:c0 + bw], wv[:, j * 1024 + kc * 512:j * 1024 + (kc + 1) * 512],
                       kc == 0, kc == 1, [wvb, CKVNb], [psb])
                ACOPY(vms[:bw, j * 512:(j + 1) * 512], ps[:bw, :], [psb], [vmsb])
            P.dma("sp", vml_s[a][t0 + c0:t0 + c0 + bw, :], vms[:bw, :], r=[vmsb], w=[vmlb[a][ti][bi]])
        SC = 96.0 ** -0.5
        kblocks = [(0, 16, 0, 0)]
        for tj in range(1, ti + 1):
            for bj in range(4):
                kblocks.append((NMETA + (tj - 1) * TT + bj * 128, 128, tj, bj))
        groups = [kblocks[i:i + 8] for i in range(0, len(kblocks), 8)]
        nkb = len(kblocks)
        flat = []
        for c in range(8):
            idx = 0
            for grp in groups:
                for si, blk in enumerate(grp):
                    for e in range(2):
                        flat.append(dict(c=c, grp=grp, si=si, blk=blk, e=e, idx=idx, first=(si == 0 and e == 0)))
                    idx += 1
        mbase = actr[2]
        actr[2] += len(flat)
        gstate = {}
        gorder = []
        for st_ in flat:
            if st_["first"]:
                gorder.append((st_["c"], st_["grp"]))
        gpos = {(c_, id(g_)): n_ for n_, (c_, g_) in enumerate(gorder)}

        def load_group(c, grp):
            g_ = actr[1]; actr[1] += 1
            kl, klb = KL[g_ % 2], KLb[g_ % 2]
            vl, vlb = VL[g_ % 2], VLb[g_ % 2]
            ks, ke = grp[0][0], grp[-1][0] + grp[-1][1]
            tjs = sorted(set(b[2] for b in grp))
            for e in range(2):
                P.dma("sp", kl[0:96, e, 0:ke - ks], kcat_s[a][2 * c + e, :, ks:ke],
                      r=[kcatb[a][tj][2 * c + e] for tj in tjs], w=[klb])
            si0 = 0
            if grp[0][1] == 16:
                P.dma("sp", vl[:16, 0, :], vml_s[a][0:16, c * 128:(c + 1) * 128], r=[vmlb[a][0][0]], w=[vlb])
                si0 = 1
            if len(grp) > si0:
                kf = grp[si0][0]
                nb = len(grp) - si0
                P.dma("sp", vl[:, si0:si0 + nb, :],
                      vml_s[a][kf:kf + nb * 128, c * 128:(c + 1) * 128].rearrange("(b p) v -> p b v", p=128),
                      r=[vmlb[a][b[2]][b[3]] for b in grp[si0:]], w=[vlb])
            return kl, klb, vl, vlb, ks

        def mla_front(i):
            st = flat[i]
            c, grp, si, e = st["c"], st["grp"], st["si"], st["e"]
            if st["first"] and (c, id(grp)) not in gstate:
                gstate[(c, id(grp))] = load_group(c, grp)
            kl, klb, vl, vlb, ks = gstate[(c, id(grp))]
            k0, kw, tj, bj = st["blk"]
            diag = (tj == ti)
            q0 = bj * 128 if (diag and ti > 0) else 0
            h = 2 * c + e
            k = mbase + i
            bsel = (0, 1, 6)[k % 3]
            ps, psb = PS[bsel], PSb[bsel]
            pt, ptb = PTM[k % 3], PTMb[k % 3]
            MM(ps[:kw, q0:w], kl[0:96, e, k0 - ks:k0 - ks + kw], QAC[0:96, h, q0:w], True, True,
               [klb, QACb[h]], [psb])
            ACTF(pt[:kw, q0:w], ps[:kw, q0:w], AF.Exp, [psb], [ptb], scale=SC)
            if diag:
                VTT(pt[:kw, q0:q0 + bw], pt[:kw, q0:q0 + bw], TRI2[:kw, 0:bw], ALU.mult, [ptb, CONSb], [ptb])

        def mla_back(i):
            st = flat[i]
            c, grp, si, e = st["c"], st["grp"], st["si"], st["e"]
            kl, klb, vl, vlb, ks = gstate[(c, id(grp))]
            k0, kw, tj, bj = st["blk"]
            diag = (tj == ti)
            q0 = bj * 128 if (diag and ti > 0) else 0
            k = mbase + i
            pt, ptb = PTM[k % 3], PTMb[k % 3]
            po, pob = PS[2 + 2 * (c % 2)], PSb[2 + 2 * (c % 2)]
            pd, pdb = PS[3 + 2 * (c % 2)], PSb[3 + 2 * (c % 2)]
            rs = slice(e * 64, (e + 1) * 64)
            MM(po[rs, q0:w], vl[:kw, si, e * 64:(e + 1) * 64], pt[:kw, q0:w], st["idx"] == 0, st["idx"] == nkb - 1,
               [vlb, ptb], [pob])
            MM(pd[rs, q0:w], ONES[:kw, 0:64], pt[:kw, q0:w], st["idx"] == 0, st["idx"] == nkb - 1,
               [ONESb, ptb], [pdb])
            if st["first"]:
                n_ = gpos[(c, id(grp))] + 1
                if n_ < len(gorder):
                    c2, g2 = gorder[n_]
                    if (c2, id(g2)) not in gstate:
                        gstate[(c2, id(g2))] = load_group(c2, g2)
            if st["idx"] == nkb - 1 and e == 1:
                VRECIP(TD[:, :w], pd[:, :w], [pdb], [TDb])
                VTT(Gv(8 + c, w), po[:, :w], TD[:, :w], ALU.mult, [pob, TDb], [Gb[8 + c]])

        LA = 2
        for i in range(len(flat)):
            if PIPE_M:
                if i == 0:
                    for j_ in range(min(LA, len(flat))):
                        mla_front(j_)
                if i + LA < len(flat):
                    mla_front(i + LA)
            else:
                mla_front(i)
            mla_back(i)
        for j in range(KC):
            wt, wtb = next_mw()
            P.dma("sp", wt[:, :], awout_b[a][j], r=[awoutb_b[a][j]], w=[wtb])
            py, pyb = PS[j % 2], PSb[j % 2]
            for kc in range(KC):
                MM(py[:, :w], wt[:, kc * 128:(kc + 1) * 128], Gv(kc, w), kc == 0, kc == KC - 1, [wtb, Gb[kc]], [pyb])
            VTT(X[:, j, :w], X[:, j, :w], py[:, :w], ALU.add, [Xb[j], pyb], [Xb[j]])

    hs_v = hsT.rearrange("(kc p) t -> p kc t", p=128)
    x_v = xT.rearrange("(kc p) t -> p kc t", p=128)
    m_v = metaT.rearrange("(kc p) t -> p kc t", p=128)
    o_v = outT.rearrange("(kc p) t -> p kc t", p=128)

    XG = [(0, 4), (4, 8), (8, 12), (12, 14), (14, 15), (15, 16)]
    hs_b = [[Buf("hs%d_%d" % (i, q)) for q in range(len(XG))] for i in range(len(cfg.tiles))]
    def emit_xload(l, ti, q):
        t0, w = cfg.tiles[ti]
        k0_, k1_ = XG[q]
        ks = slice(k0_, k1_)
        if l == 0:
            src = m_v[:, ks, :] if ti == 0 else x_v[:, ks, t0 - NMETA:t0 - NMETA + w]
            P.dma("sp", X[:, ks, :w], src, w=Xb[k0_:k1_])
        else:
            P.dma("sp", X[:, ks, :w], hs_v[:, ks, t0:t0 + w], r=[hs_b[ti][q]], w=Xb[k0_:k1_])

    seq = [(l, ti) for l in range(L) for ti in range(len(cfg.tiles))]
    early = set()
    ri = 0
    ai = 0
    for idx, (l, ti) in enumerate(seq):
        lt = cfg.layers[l]
        t0, w = cfg.tiles[ti]
        if True:
            if (l, ti) not in early:
                for q in range(len(XG)):
                    emit_xload(l, ti, q)
            if l + 1 < L and PACE_ON:
                nt_ = len(cfg.tiles)
                nj = len(cast_jobs[l + 1])
                lo_, hi_ = (nj * ti) // nt_, (nj * (ti + 1)) // nt_
                if hi_ > lo_:
                    pb = Buf("pace%d_%d" % (l, ti))
                    VMEMSET(PACE[:, :], 0.0, [pb])
                    emit_casts(l + 1, lo_, hi_, pb)
            if lt == "r":
                rec_tile(l, ri, ti, w)
            if lt == "a":
                if ti == 0:
                    att_prep(ai)
                att_tile(l, ai, ti, t0, w)
            if l < L - 1:
                nxt = seq[idx + 1] if idx + 1 < len(seq) else None
                if nxt is not None:
                    early.add(nxt)

                def store_chunk(j, ti=ti, t0=t0, w=w, nxt=nxt):
                    for q, (k0_, k1_) in enumerate(XG):
                        if j == k1_ - 1:
                            ks = slice(k0_, k1_)
                            P.dma("sp", hs_v[:, ks, t0:t0 + w], X[:, ks, :w], r=Xb[k0_:k1_], w=[hs_b[ti][q]])
                            if nxt is not None:
                                emit_xload(nxt[0], nxt[1], q)
                ffn_tile(l, w, store_chunk)
            else:
                ffn_tile(l, w)
                if ti > 0:
                    gc = 2 * L * KC
                    rms_rstd(w, 1.0 / D, [(X[:, kc, :w], [Xb[kc]]) for kc in range(KC)], EPSC[:, 0:1], EPSCb)
                    for kc in range(KC):
                        VSTT(X[:, kc, :w], X[:, kc, :w], GN[:, gc + kc:gc + kc + 1], RSTD[:, :w], ALU.mult, ALU.mult,
                             [Xb[kc], GNb, RSTDb], [Xb[kc]])
                        if kc % 4 == 3:
                            q = kc // 4
                            ks = slice(q * 4, q * 4 + 4)
                            P.dma("sp", o_v[:, ks, t0 - NMETA:t0 - NMETA + w], X[:, ks, :w], r=Xb[q * 4:q * 4 + 4],
                                  w=[Buf("o")], is_out=True)
        if ti == len(cfg.tiles) - 1:
            if lt == "r":
                ri += 1
            if lt == "a":
                ai += 1
    P.finish()
    return nc


def prep_shared(inp, cfg):
    L = cfg.depth
    sh = {}
    gains = [col_layout(np.asarray(inp["mix_norm"][l])) for l in range(L)]
    gains += [col_layout(np.asarray(inp["ffn_norm"][l])) for l in range(L)]
    gains += [col_layout(np.asarray(inp["final_norm"]))]
    sh["gains"] = np.ascontiguousarray(np.concatenate(gains, axis=1), dtype=np.float32)
    cw = np.zeros((128, L, FC, 4), np.float32)
    for l in range(L):
        for j in range(3):
            cw[:, l, :, j] = col_layout(np.asarray(inp["ffn_conv_w"][l, j]))
        cw[:, l, :, 3] = col_layout(np.asarray(inp["ffn_conv_b"][l]))
    sh["convw"] = cw.reshape(128, L * FC * 4)
    for l in range(L):
        sh["wu%d" % l] = tile_w(np.asarray(inp["ffn_w_up"][l]), 128).reshape(FC, 128, KC * 128)
        sh["wg%d" % l] = tile_w(np.asarray(inp["ffn_w_gate"][l]), 128).reshape(FC, 128, KC * 128)
        sh["wd%d" % l] = tile_w(np.asarray(inp["ffn_w_down"][l]), 128).reshape(KC, 128, FC * 128)
    sh["metaT"] = np.ascontiguousarray(np.asarray(inp["meta_tokens"]).T)
    s_i = np.arange(128)[:, None]
    t_i = np.arange(128)[None, :]
    maskbd = (((s_i // 64) == (t_i // 64)) & (s_i <= t_i)) | ((s_i < 64) & (t_i >= 64))
    ident = np.eye(128)
    col = np.arange(512)[None, :]
    reset64 = np.broadcast_to((col % 64 != 0), (128, 512))
    reset128 = np.broadcast_to((col % 128 != 0), (128, 512))
    tri = (s_i <= t_i)
    strict = (s_i > t_i)
    prev16 = np.zeros((128, 128), bool)
    prev16[0:16, :] = (t_i < 112 + np.arange(16)[:, None])
    sh["consts"] = np.ascontiguousarray(np.concatenate(
        [maskbd, ident, reset64, reset128, tri, tri, strict, strict, prev16, prev16], axis=1).astype(np.float32))
    NA = max(cfg.n_att, 1)
    anorm = np.zeros((128, NA * 6), np.float32)
    sinkc = np.zeros((128, NA * 8), np.float32)
    for a in range(cfg.n_att):
        w_in = np.asarray(inp["att_w_in"][a])
        tl = [tile_w(w_in[:, 0:1024], 128), tile_w(w_in[:, 1024:1280], 128), tile_w(w_in[:, 1280:1536], 128),
              tile_w(w_in[:, 1536:2048], 128), tile_w(w_in[:, 2048:2304], 128)]
        sh["awin%d" % a] = np.ascontiguousarray(np.concatenate(tl, axis=0).reshape(18, 128, KC * 128))
        sh["awkr%d" % a] = tile_w(w_in[:, 2304:2336], 32).reshape(128, KC * 32)
        wuq = tile_w(np.asarray(inp["mla_w_uq"][a]), 96)
        sh["wuq%d" % a] = np.ascontiguousarray(
            wuq.reshape(4, 4, 128, 384).transpose(0, 2, 1, 3).reshape(4, 128, 1536))
        wukv = np.asarray(inp["mla_w_ukv"][a]).reshape(256, 16, 128)
        wk_ = np.ascontiguousarray(wukv[:, :, 0:64].reshape(256, 1024))
        wv_ = np.ascontiguousarray(wukv[:, :, 64:128].reshape(256, 1024))
        sh["wukvk%d" % a] = np.ascontiguousarray(tile_w(wk_, 128).transpose(1, 0, 2, 3).reshape(128, 2048))
        sh["wukvv%d" % a] = np.ascontiguousarray(tile_w(wv_, 512).transpose(1, 0, 2, 3).reshape(128, 2048))
        sh["awout%d" % a] = tile_w(np.asarray(inp["att_w_out"][a]), 128).reshape(KC, 128, KC * 128)
        anorm[:, a * 6:a * 6 + 4] = col_layout(np.asarray(inp["mla_q_norm"][a]))
        anorm[:, a * 6 + 4:a * 6 + 6] = col_layout(np.asarray(inp["mla_kv_norm"][a]))
        sk = np.asarray(inp["att_sinks"][a])
        sinkc[0:64, a * 8:(a + 1) * 8] = sk[0::2][None, :]
        sinkc[64:128, a * 8:(a + 1) * 8] = sk[1::2][None, :]
    sh["anorm"] = anorm
    sh["sinkc"] = sinkc
    half = 16
    inv_freq = (np.float32(10000.0) ** (np.float32(-2.0) * np.arange(half, dtype=np.float32) / np.float32(32))).astype(np.float32)
    ang = (np.arange(cfg.T).astype(np.float32)[:, None] * inv_freq[None, :]).astype(np.float32)
    cs, sn = np.cos(ang).astype(np.float32), np.sin(ang).astype(np.float32)
    sh["ropec"] = np.ascontiguousarray(np.concatenate([cs, cs], axis=1).T)
    sh["ropes"] = np.ascontiguousarray(np.concatenate([sn, sn], axis=1).T)
    NR = max(cfg.n_rec, 1)
    lbraw = np.zeros((128, NR * 16), np.float32)
    ong = np.zeros((128, NR), np.float32)
    for r in range(cfg.n_rec):
        sh["rwin%d" % r] = tile_w(np.asarray(inp["rec_w_in"][r]), 128).reshape(64, 128, KC * 128)
        sh["rwout%d" % r] = tile_w(np.asarray(inp["rec_w_out"][r]), 128).reshape(KC, 128, KC * 128)
        lbraw[:, r * 16:(r + 1) * 16] = col_layout(np.asarray(inp["rec_lower_bounds"][r]))
        ong[:, r] = np.asarray(inp["rec_out_norm"][r])
    sh["lbraw"] = lbraw
    sh["ong"] = ong
    return sh


def run(inp, cfg, ncores=8):
    x = np.asarray(inp["x"])
    B = x.shape[0]
    sh = prep_shared(inp, cfg)
    nc = build(cfg)
    if ncores == 8 and B == 4:
        active = [0, 1, 4, 5]
    else:
        active = list(range(min(B, ncores)))
    zero = None
    in_maps = []
    for c in range(ncores):
        if c in active:
            m = dict(sh)
            m["xT"] = np.ascontiguousarray(x[active.index(c)].T)
        else:
            if zero is None:
                zero = {k: np.zeros_like(v) for k, v in sh.items()}
                zero["xT"] = np.zeros((D, cfg.t_real), np.float32)
            m = zero
        in_maps.append(m)
    res = run_bass_kernel_spmd(nc, in_maps, core_ids=list(range(ncores)))
    out = np.stack([np.ascontiguousarray(res.results[active[b]]["outT"].T) for b in range(len(active))], axis=0)
    return out.astype(np.float32)


def kernel(**inputs):
    cfg = Cfg(4096, 4)
    return run(inputs, cfg, ncores=8)
```

```python
import numpy as np
import concourse.bass as bass
import concourse.mybir as mybir
from concourse.bass_utils import run_bass_kernel_spmd

F32 = mybir.dt.float32
BF16 = mybir.dt.bfloat16
AF = mybir.ActivationFunctionType
ALU = mybir.AluOpType

D = 2048
KC = 16
NMETA = 16
DFF = 5632
FC = 44
EPS = 1e-6
TT = 512
import os as _os
PIPE_R = _os.environ.get('PIPE_R', '1') == '1'
PIPE_S = _os.environ.get('PIPE_S', '1') == '1'
PIPE_M = _os.environ.get('PIPE_M', '1') == '1'
PACE_ON = _os.environ.get('PACE_ON', '1') == '1'
HCHUNK = _os.environ.get('HCHUNK', '1') == '1'
PSQ_ON = False


class Buf:
    __slots__ = ("name", "arena", "lo", "hi", "last_w", "rd", "rd_dma")

    def __init__(self, name, arena=None, lo=0, hi=1):
        self.name = name
        self.arena = arena if arena is not None else [self]
        if arena is not None:
            arena.append(self)
        self.lo, self.hi = lo, hi
        self.last_w = None
        self.rd = {}
        self.rd_dma = []


class Op:
    __slots__ = ("eng", "fn", "dma", "pos", "signal", "waits", "slot", "val", "cnt")

    def __init__(self, eng, fn, dma):
        self.eng, self.fn, self.dma = eng, fn, dma
        self.pos = 0
        self.signal = False
        self.waits = []
        self.slot = None
        self.val = 0
        self.cnt = 0


ENGS = ("pe", "act", "dve", "pool", "sp")
SEM_CH = 12000
NSLOT = 40


class Prog:
    def __init__(self, nc):
        self.nc = nc
        self.streams = {e: [] for e in ENGS}
        self.waited = {e: {f: -1 for f in ENGS} for e in ENGS}
        self.waited_dma = {e: {} for e in ENGS}
        self.slot_last = {}
        self.ndma = {e: 0 for e in ENGS}
        self.out_dmas = []

    def _overl(self, b):
        if len(b.arena) == 1:
            return b.arena
        return [r for r in b.arena if r.lo < b.hi and b.lo < r.hi]

    def _add(self, eng, fn, r, w, dma):
        op = Op(eng, fn, dma)
        st = self.streams[eng]
        op.pos = len(st)
        deps = {}
        for b in r:
            for q in self._overl(b):
                if q.last_w is not None:
                    deps[id(q.last_w)] = q.last_w
        for b in w:
            for q in self._overl(b):
                if q.last_w is not None:
                    deps[id(q.last_w)] = q.last_w
                for d in q.rd.values():
                    deps[id(d)] = d
                for d in q.rd_dma:
                    deps[id(d)] = d
        if dma:
            slot = (eng, self.ndma[eng] % NSLOT)
            self.ndma[eng] += 1
            prev = self.slot_last.get(slot)
            if prev is not None:
                deps[id(prev)] = prev
            op.slot = slot
            op.val = (prev.val if prev is not None else 0) + 16
            self.slot_last[slot] = op
        wd = self.waited_dma[eng]
        wc = self.waited[eng]
        for d in deps.values():
            if d is op:
                continue
            if d.dma:
                if wd.get(d.slot, 0) >= d.val:
                    continue
                wd[d.slot] = d.val
                op.waits.append(d)
            else:
                if d.eng == eng and (eng == "pe" or dma):
                    if eng == "pe":
                        continue
                if wc[d.eng] >= d.pos:
                    continue
                wc[d.eng] = d.pos
                d.signal = True
                op.waits.append(d)
        for b in r:
            if dma:
                b.rd_dma.append(op)
            else:
                b.rd[eng] = op
        for b in w:
            b.last_w = op
            b.rd = {}
            b.rd_dma = []
        st.append(op)
        return op

    def op(self, eng, fn, r=(), w=()):
        return self._add(eng, fn, r, w, False)

    def dma(self, eng, out, in_, r=(), w=(), is_out=False):
        o = self._add(eng, lambda e: e.dma_start(out=out, in_=in_), r, w, True)
        if is_out:
            self.out_dmas.append(o)
        return o

    def finish(self):
        nc = self.nc
        sems = {}

        def getsem(key):
            if key not in sems:
                sems[key] = nc.alloc_semaphore("s_%s_%s" % (key[0], key[1]))
            return sems[key]

        for e in ENGS:
            c = 0
            for o in self.streams[e]:
                if o.signal and not o.dma:
                    o.cnt = c
                    c += 1
        handles = {"pe": "tensor", "act": "scalar", "dve": "vector", "pool": "gpsimd", "sp": "sync"}
        out_dmas = self.out_dmas

        def emit(ename, e):
            for o in self.streams[ename]:
                for d in o.waits:
                    if d.dma:
                        e.wait_ge(getsem(("d" + d.slot[0], d.slot[1])), d.val)
                    else:
                        e.wait_ge(getsem((d.eng, d.cnt // SEM_CH)), d.cnt % SEM_CH + 1)
                ins = o.fn(e)
                if o.dma:
                    ins.then_inc(getsem(("d" + o.slot[0], o.slot[1])), 16)
                elif o.signal:
                    ins.then_inc(getsem((o.eng, o.cnt // SEM_CH)), 1)
            if ename == "sp":
                for d in out_dmas:
                    e.wait_ge(getsem(("d" + d.slot[0], d.slot[1])), d.val)

        with nc.Block() as block:
            for ename in ENGS:
                getattr(block, handles[ename])(lambda e, _n=ename: emit(_n, e))


def tile_w(W, M):
    K, N = W.shape
    return np.ascontiguousarray(W.reshape(K // 128, 128, N // M, M).transpose(2, 1, 0, 3))


def col_layout(v):
    return np.ascontiguousarray(v.reshape(-1, 128).T)


class Cfg:
    def __init__(self, t_real, depth, mixers=True, layers=None):
        self.t_real = t_real
        self.depth = depth
        self.mixers = mixers
        if layers is None:
            layers = ["a" if (l % 2 == 0) else "r" for l in range(depth)] if mixers else ["n"] * depth
        self.layers = layers
        self.n_att = sum(1 for t in layers if t == "a")
        self.n_rec = sum(1 for t in layers if t == "r")
        self.T = NMETA + t_real
        self.tiles = [(0, NMETA)] + [(NMETA + i * TT, TT) for i in range(t_real // TT)]


def build(cfg):
    nc = bass.Bass("TRN2", target_bir_lowering=False)
    P = Prog(nc)
    L = cfg.depth
    T = cfg.T
    NR = max(cfg.n_rec, 1)
    NA = max(cfg.n_att, 1)

    def din(name, shape, dt=F32):
        return nc.dram_tensor(name, list(shape), dt, kind="ExternalInput").ap()

    def dscr(name, shape, dt=BF16):
        return nc.dram_tensor(name, list(shape), dt, kind="Internal").ap()

    def sb(name, shape, dt=F32):
        return nc.alloc_sbuf_tensor(name, list(shape), dt)

    xT = din("xT", [D, cfg.t_real])
    metaT = din("metaT", [D, NMETA])
    outT = nc.dram_tensor("outT", [D, cfg.t_real], F32, kind="ExternalOutput").ap()
    hsT = dscr("hsT", [D, T], F32)
    gains_d = din("gains", [128, (2 * L + 1) * KC])
    convw_d = din("convw", [128, L * FC * 4])
    consts_d = din("consts", [128, 2048])
    wu_f = [din("wu%d" % l, [FC, 128, KC * 128]) for l in range(L)]
    wg_f = [din("wg%d" % l, [FC, 128, KC * 128]) for l in range(L)]
    wd_f = [din("wd%d" % l, [KC, 128, FC * 128]) for l in range(L)]
    wu_b = [dscr("wub%d" % l, [FC, 128, KC * 128]) for l in range(L)]
    wg_b = [dscr("wgb%d" % l, [FC, 128, KC * 128]) for l in range(L)]
    wd_b = [dscr("wdb%d" % l, [KC, 128, FC * 128]) for l in range(L)]
    rwin_f = [din("rwin%d" % r, [64, 128, KC * 128]) for r in range(cfg.n_rec)]
    rwout_f = [din("rwout%d" % r, [KC, 128, KC * 128]) for r in range(cfg.n_rec)]
    rwin_b = [dscr("rwinb%d" % r, [64, 128, KC * 128]) for r in range(cfg.n_rec)]
    rwout_b = [dscr("rwoutb%d" % r, [KC, 128, KC * 128]) for r in range(cfg.n_rec)]
    lbraw_d = din("lbraw", [128, NR * 16])
    ong_d = din("ong", [128, NR])
    awin_f = [din("awin%d" % a, [18, 128, KC * 128]) for a in range(cfg.n_att)]
    awin_b = [dscr("awinb%d" % a, [18, 128, KC * 128]) for a in range(cfg.n_att)]
    awkr_f = [din("awkr%d" % a, [128, KC * 32]) for a in range(cfg.n_att)]
    awkr_b = [dscr("awkrb%d" % a, [128, KC * 32]) for a in range(cfg.n_att)]
    awkrrot_b = [dscr("awkrrotb%d" % a, [128, KC * 32]) for a in range(cfg.n_att)]
    wuq_f = [din("wuq%d" % a, [4, 128, 4 * 384]) for a in range(cfg.n_att)]
    wuq_b = [dscr("wuqb%d" % a, [4, 128, 4 * 384]) for a in range(cfg.n_att)]
    wrot_b = [dscr("wrotb%d" % a, [128, 2048]) for a in range(cfg.n_att)]
    wukvk_f = [din("wukvk%d" % a, [128, 2048]) for a in range(cfg.n_att)]
    wukvk_b = [dscr("wukvkb%d" % a, [128, 2048]) for a in range(cfg.n_att)]
    wukvv_f = [din("wukvv%d" % a, [128, 2048]) for a in range(cfg.n_att)]
    wukvv_b = [dscr("wukvvb%d" % a, [128, 2048]) for a in range(cfg.n_att)]
    awout_f = [din("awout%d" % a, [KC, 128, KC * 128]) for a in range(cfg.n_att)]
    awout_b = [dscr("awoutb%d" % a, [KC, 128, KC * 128]) for a in range(cfg.n_att)]
    anorm_d = din("anorm", [128, NA * 6])
    sinkc_d = din("sinkc", [128, NA * 8])
    ropec_d = din("ropec", [32, T])
    ropes_d = din("ropes", [32, T])
    kcat_s = [dscr("kcat%d" % a, [16, 96, T]) for a in range(cfg.n_att)]
    vml_s = [dscr("vml%d" % a, [T, 1024]) for a in range(cfg.n_att)]

    X = sb("X", [128, KC, TT]); Xb = [Buf("X%d" % i) for i in range(KC)]
    H = sb("H", [128, KC, TT], BF16); Hb = [Buf("H%d" % i) for i in range(KC)]
    if not HCHUNK:
        Hb = [Hb[0]] * KC
    GA = sb("GA", [128, FC * TT], BF16)
    garena = []
    Gb = [Buf("G%d" % i, garena, i * TT * 2, (i + 1) * TT * 2) for i in range(FC)]

    def Gv(fc, w):
        return GA[:, fc * TT:fc * TT + w]

    def ga_f32(byte_off, n):
        return GA[:, byte_off // 2: byte_off // 2 + 2 * n].bitcast(F32)

    NWA = 4
    WA = [sb("WA%d" % i, [128, KC * 128], BF16) for i in range(NWA)]; WAb = [Buf("WA%d" % i) for i in range(NWA)]
    WD = [sb("WD%d" % i, [128, FC * 128], BF16) for i in range(2)]
    WDar = [[], []]
    WDb = [Buf("WD%d" % i, WDar[i], 0, FC * 128) for i in range(2)]
    WDh = [(WD[i][:, hh * 2048:(hh + 1) * 2048], Buf("WD%d_%d" % (i, hh), WDar[i], hh * 2048, (hh + 1) * 2048))
           for i in range(2) for hh in range(2)]
    SQ = [sb("SQ%d" % i, [128, TT], BF16) for i in range(2)]; SQb = [Buf("SQ%d" % i) for i in range(2)]
    RSTD = sb("RSTD", [128, TT]); RSTDb = Buf("RSTD")
    TMPN = sb("TMPN", [128, TT]); TMPNb = Buf("TMPN")
    ASB = [sb("ASB%d" % i, [128, TT + 2]) for i in range(2)]; ASBb = [Buf("ASB%d" % i) for i in range(2)]
    ACC = [sb("ACC%d" % i, [128, TT]) for i in range(2)]; ACCb = [Buf("ACC%d" % i) for i in range(2)]
    CC = sb("CC", [128, L, FC, 2]); CCb = [Buf("CC%d" % l) for l in range(L)]
    GN = sb("GN", [128, (2 * L + 1) * KC]); GNb = Buf("GN")
    CW = sb("CW", [128, L * FC * 4]); CWb = Buf("CW")
    CONS = sb("CONS", [128, 2048]); CONSb = Buf("CONS")
    TRI2 = CONS[:, 1280:1536]
    STRICT2 = CONS[:, 1536:1792]
    PREV16_2 = CONS[:, 1792:2048]
    MASKBD = CONS[:, 0:128]
    RESET64 = CONS[:, 256:768]
    RESET128 = CONS[:, 768:1280]
    ONES = sb("ONES", [128, 128], BF16); ONESb = Buf("ONES")
    IDENT = sb("IDENT", [128, 128], BF16); IDENTb = Buf("IDENT")
    EPSC = sb("EPSC", [128, 1]); EPSCb = Buf("EPSC")
    ONEC = sb("ONEC", [128, 1]); ONECb = Buf("ONEC")
    PS = [nc.alloc_psum_tensor("PS%d" % i, [128, 512], F32) for i in range(7)]
    PSar = [[] for i in range(7)]
    PSb = [Buf("PS%d" % i, PSar[i], 0, 512) for i in range(7)]
    PSq = [[Buf("PS%d_%d" % (i, q), PSar[i], q * 128, (q + 1) * 128) for q in range(4)] for i in range(7)]
    if not PSQ_ON:
        PSq = [[PSb[i]] * 4 for i in range(7)]
    PST = nc.alloc_psum_tensor("PST", [128, 1024], BF16); PSTb = Buf("PST")

    def MM(out, lhsT, rhs, start, stop, r, w):
        P.op("pe", lambda e: e.matmul(out, lhsT=lhsT, rhs=rhs, start=start, stop=stop), r=r, w=w)

    def ACTF(out, in_, func, r, w, **kw):
        P.op("act", lambda e: e.activation(out=out, in_=in_, func=func, **kw), r=r, w=w)

    def ACOPY(out, in_, r, w):
        P.op("act", lambda e: e.copy(out=out, in_=in_), r=r, w=w)

    def VTT(out, in0, in1, op, r, w):
        P.op("dve", lambda e: e.tensor_tensor(out=out, in0=in0, in1=in1, op=op), r=r, w=w)

    def VTS(out, in0, s1, s2, op0, op1, r, w):
        if op1 is None:
            P.op("dve", lambda e: e.tensor_scalar(out=out, in0=in0, scalar1=s1, scalar2=None, op0=op0), r=r, w=w)
        else:
            P.op("dve", lambda e: e.tensor_scalar(out=out, in0=in0, scalar1=s1, scalar2=s2, op0=op0, op1=op1),
                 r=r, w=w)

    def VSTT(out, in0, scalar, in1, op0, op1, r, w):
        P.op("dve", lambda e: e.scalar_tensor_tensor(out=out, in0=in0, scalar=scalar, in1=in1, op0=op0, op1=op1),
             r=r, w=w)

    def VCOPY(out, in_, r, w):
        P.op("dve", lambda e: e.tensor_copy(out=out, in_=in_), r=r, w=w)

    def VRECIP(out, in_, r, w):
        P.op("dve", lambda e: e.reciprocal(out=out, in_=in_), r=r, w=w)

    def VMEMSET(ap, val, w):
        P.op("dve", lambda e: e.memset(ap, val), w=w)

    P.dma("sp", GN[:, :], gains_d[:, :], w=[GNb])
    P.dma("sp", CW[:, :], convw_d[:, :], w=[CWb])
    P.dma("sp", CONS[:, :], consts_d[:, :], w=[CONSb])
    VMEMSET(ONES[:, :], 1.0, [ONESb])
    VMEMSET(EPSC[:, :], EPS, [EPSCb])
    VMEMSET(ONEC[:, :], 1.0, [ONECb])
    VMEMSET(CC[:, :, :, :], 0.0, CCb)
    VCOPY(IDENT[:, :], CONS[:, 128:256], [CONSb], [IDENTb])

    LBR = sb("LBR", [128, NR * 16]); LBRb = Buf("LBR")
    LBE = sb("LBE", [128, NR * 16]); LBEb = Buf("LBE")
    LB = sb("LB", [128, NR * 16]); LBb = Buf("LB")
    OML = sb("OML", [128, NR * 16]); OMLb = Buf("OML")
    LBM = sb("LBM", [128, 16]); LBMb = Buf("LBM")
    LBS = sb("LBS", [128, 16]); LBSb = Buf("LBS")
    ONG = sb("ONG", [128, NR]); ONGb = Buf("ONG")
    EPSR = sb("EPSR", [128, 1]); EPSRb = Buf("EPSR")
    if cfg.n_rec > 0:
        P.dma("sp", LBR[:, :], lbraw_d[:, :], w=[LBRb])
        P.dma("sp", ONG[:, :], ong_d[:, :], w=[ONGb])
        VCOPY(LBM[:, :], LBR[:, 0:16], [LBRb], [LBMb])
        for r in range(1, NR):
            VTT(LBM[:, :], LBM[:, :], LBR[:, r * 16:(r + 1) * 16], ALU.max, [LBMb, LBRb], [LBMb])
        for r in range(NR):
            VTT(LBE[:, r * 16:(r + 1) * 16], LBR[:, r * 16:(r + 1) * 16], LBM[:, :], ALU.subtract,
                [LBRb, LBMb], [LBEb])
        ACTF(LBE[:, :], LBE[:, :], AF.Exp, [LBEb], [LBEb])
        VCOPY(LBS[:, :], LBE[:, 0:16], [LBEb], [LBSb])
        for r in range(1, NR):
            VTT(LBS[:, :], LBS[:, :], LBE[:, r * 16:(r + 1) * 16], ALU.add, [LBSb, LBEb], [LBSb])
        VRECIP(LBS[:, :], LBS[:, :], [LBSb], [LBSb])
        VMEMSET(LB[:, 0:16], 0.0, [LBb])
        for r in range(1, NR):
            VTT(LBE[:, r * 16:(r + 1) * 16], LBE[:, r * 16:(r + 1) * 16], LBS[:, :], ALU.mult, [LBEb, LBSb], [LBEb])
            VTT(LB[:, r * 16:(r + 1) * 16], LB[:, (r - 1) * 16:r * 16], LBE[:, r * 16:(r + 1) * 16], ALU.add,
                [LBb, LBEb], [LBb])
        VTS(OML[:, :], LB[:, :], -1.0, 1.0, ALU.mult, ALU.add, [LBb], [OMLb])

    wub_b = [[Buf("wub%d_%d" % (l, i)) for i in range(FC)] for l in range(L)]
    wgb_b = [[Buf("wgb%d_%d" % (l, i)) for i in range(FC)] for l in range(L)]
    wdb_b = [[Buf("wdb%d_%d" % (l, i)) for i in range(KC)] for l in range(L)]
    rwinb_b = [[Buf("rwin%d_%d" % (r, i)) for i in range(64)] for r in range(cfg.n_rec)]
    rwoutb_b = [[Buf("rwout%d_%d" % (r, i)) for i in range(KC)] for r in range(cfg.n_rec)]

    cast_jobs = [[] for _ in range(L)]
    cur_l = [0]

    class _CastQ:
        def dma(self, q, out, in_, w):
            cast_jobs[cur_l[0]].append((out, in_, w))
    PQ = _CastQ()

    def cast_ffn(l):
        for i in range(FC):
            PQ.dma("pool", wu_b[l][i], wu_f[l][i], w=[wub_b[l][i]])
            PQ.dma("pool", wg_b[l][i], wg_f[l][i], w=[wgb_b[l][i]])
        for i in range(KC):
            PQ.dma("pool", wd_b[l][i], wd_f[l][i], w=[wdb_b[l][i]])

    def cast_rec(r):
        for hd in range(16):
            for which in range(4):
                i = which * 16 + hd
                PQ.dma("pool", rwin_b[r][i], rwin_f[r][i], w=[rwinb_b[r][i]])
        for i in range(KC):
            PQ.dma("pool", rwout_b[r][i], rwout_f[r][i], w=[rwoutb_b[r][i]])

    awinb_b = [[Buf("awin%d_%d" % (a, i)) for i in range(18)] for a in range(cfg.n_att)]
    awkrb_b = [Buf("awkr%d" % a) for a in range(cfg.n_att)]
    awkrrotb_b = [Buf("awkrrot%d" % a) for a in range(cfg.n_att)]
    wuqb_b = [[Buf("wuq%d_%d" % (a, i)) for i in range(4)] for a in range(cfg.n_att)]
    wrotb_b = [Buf("wrot%d" % a) for a in range(cfg.n_att)]
    wukvkb_b = [Buf("wukvk%d" % a) for a in range(cfg.n_att)]
    wukvvb_b = [Buf("wukvv%d" % a) for a in range(cfg.n_att)]
    awoutb_b = [[Buf("awout%d_%d" % (a, i)) for i in range(KC)] for a in range(cfg.n_att)]

    def cast_att(a):
        for i in range(18):
            PQ.dma("pool", awin_b[a][i], awin_f[a][i], w=[awinb_b[a][i]])
        PQ.dma("pool", awkr_b[a], awkr_f[a], w=[awkrb_b[a]])
        for i in range(4):
            PQ.dma("pool", wuq_b[a][i], wuq_f[a][i], w=[wuqb_b[a][i]])
        PQ.dma("pool", wukvk_b[a], wukvk_f[a], w=[wukvkb_b[a]])
        PQ.dma("pool", wukvv_b[a], wukvv_f[a], w=[wukvvb_b[a]])
        for i in range(KC):
            PQ.dma("pool", awout_b[a][i], awout_f[a][i], w=[awoutb_b[a][i]])

    ri = 0
    ai = 0
    for l in range(L):
        cur_l[0] = l
        if cfg.layers[l] == "r":
            cast_rec(ri); ri += 1
        if cfg.layers[l] == "a":
            cast_att(ai); ai += 1
        cast_ffn(l)
    PACE = sb("PACE", [128, 1])

    def emit_casts(l, lo, hi, pace=None):
        for (out, in_, w) in cast_jobs[l][lo:hi]:
            P.dma("pool", out, in_, r=([pace] if pace is not None else []), w=w)

    emit_casts(0, 0, len(cast_jobs[0]))
    if not PACE_ON:
        for l_ in range(1, L):
            emit_casts(l_, 0, len(cast_jobs[l_]))

    wa_ctr = [0]

    mix_ctr = [0]

    def next_wa():
        i = wa_ctr[0] % NWA
        wa_ctr[0] += 1
        return WA[i], WAb[i]

    def next_mw():
        i = mix_ctr[0] % 8
        mix_ctr[0] += 1
        if i < 4:
            return WA[i], WAb[i]
        return WDh[i - 4]

    def rms_rstd(w, nfeat_inv, src_chunks, eps_ap, eps_b):
        n = len(src_chunks)
        for i, (ap, bufs) in enumerate(src_chunks):
            sq, sqb = SQ[i % 2], SQb[i % 2]
            ACTF(sq[:, :w], ap, AF.Square, bufs, [sqb])
            MM(PS[6][:, :w], ONES[:, :], sq[:, :w], i == 0, i == n - 1, [sqb, ONESb], [PSb[6]])
        ACTF(TMPN[:, :w], PS[6][:, :w], AF.Sqrt, [PSb[6], eps_b], [TMPNb], bias=eps_ap, scale=nfeat_inv)
        VRECIP(RSTD[:, :w], TMPN[:, :w], [TMPNb], [RSTDb])

    def norm_H(gcol, w):
        rms_rstd(w, 1.0 / D, [(X[:, kc, :w], [Xb[kc]]) for kc in range(KC)], EPSC[:, 0:1], EPSCb)
        for kc in range(KC):
            VSTT(H[:, kc, :w], X[:, kc, :w], GN[:, gcol + kc:gcol + kc + 1], RSTD[:, :w], ALU.mult, ALU.mult,
                 [Xb[kc], GNb, RSTDb], [Hb[kc]])

    def ffn_tile(l, w, after_chunk=None):
        norm_H((L + l) * KC, w)
        for fc in range(FC):
            wu, wub = next_wa()
            wg, wgb = next_wa()
            P.dma("sp", wu[:, :], wu_b[l][fc], r=[wub_b[l][fc]], w=[wub])
            P.dma("sp", wg[:, :], wg_b[l][fc], r=[wgb_b[l][fc]], w=[wgb])
            pu, pub = PS[fc % 2], PSb[fc % 2]
            pa, pab = PS[2 + fc % 2], PSb[2 + fc % 2]
            for kc in range(KC):
                MM(pu[:, :w], wu[:, kc * 128:(kc + 1) * 128], H[:, kc, :w], kc == 0, kc == KC - 1, [wub, Hb[kc]], [pub])
            for kc in range(KC):
                MM(pa[:, :w], wg[:, kc * 128:(kc + 1) * 128], H[:, kc, :w], kc == 0, kc == KC - 1, [wgb, Hb[kc]], [pab])
            asb, asbb = ASB[fc % 2], ASBb[fc % 2]
            acc, accb = ACC[fc % 2], ACCb[fc % 2]
            sl, slb = acc, accb
            cb = (l * FC + fc) * 4
            ACOPY(asb[:, 0:2], CC[:, l, fc, :], [CCb[l]], [asbb])
            ACOPY(asb[:, 2:2 + w], pa[:, :w], [pab], [asbb])
            ACTF(acc[:, :w], pa[:, :w], AF.Identity, [pab, CWb], [accb], scale=CW[:, cb + 2:cb + 3],
                 bias=CW[:, cb + 3:cb + 4])
            VSTT(acc[:, :w], asb[:, 1:1 + w], CW[:, cb + 1:cb + 2], acc[:, :w], ALU.mult, ALU.add,
                 [asbb, accb, CWb], [accb])
            VSTT(acc[:, :w], asb[:, 0:w], CW[:, cb:cb + 1], acc[:, :w], ALU.mult, ALU.add,
                 [asbb, accb, CWb], [accb])
            ACOPY(CC[:, l, fc, :], asb[:, w:w + 2], [asbb], [CCb[l]])
            ACTF(sl[:, :w], acc[:, :w], AF.Silu, [accb], [slb])
            VTT(Gv(fc, w), sl[:, :w], pu[:, :w], ALU.mult, [slb, pub], [Gb[fc]])
        P.dma("sp", WD[0][:, :], wd_b[l][0], r=[wdb_b[l][0]], w=[WDb[0]])
        for j in range(KC):
            wd, wdb = WD[j % 2], WDb[j % 2]
            if j + 1 < KC:
                P.dma("sp", WD[(j + 1) % 2][:, :], wd_b[l][j + 1], r=[wdb_b[l][j + 1]], w=[WDb[(j + 1) % 2]])
            py, pyb = PS[4 + j % 2], PSb[4 + j % 2]
            for fc in range(FC):
                MM(py[:, :w], wd[:, fc * 128:(fc + 1) * 128], Gv(fc, w), fc == 0, fc == FC - 1, [wdb, Gb[fc]], [pyb])
            VTT(X[:, j, :w], X[:, j, :w], py[:, :w], ALU.add, [Xb[j], pyb], [Xb[j]])
            if after_chunk is not None:
                after_chunk(j)

    NG = 4
    MA_BYTES = 48 * 1024
    MA = sb("MA", [128, MA_BYTES // 2], BF16)
    marena = []

    class Bump:
        def __init__(self, t, arena, base=0, limit=None):
            self.t, self.arena, self.off, self.limit = t, arena, base, limit

        def get(self, name, nbytes, dt=BF16, nbuf=None):
            assert nbytes % 4 == 0
            if self.limit is not None:
                assert self.off + nbytes <= self.limit, (name, self.off, nbytes, self.limit)
            ap = self.t[:, self.off // 2:(self.off + nbytes) // 2]
            if dt == F32:
                ap = ap.bitcast(F32)
            b = Buf(name, self.arena, self.off, self.off + nbytes)
            self.off += nbytes
            return ap, b

    if cfg.n_rec > 0:
        mr = Bump(MA, marena, 0, MA_BYTES)
        S32f, _ = mr.get("S32", 16 * 128 * 4, F32); marena.pop()
        S32 = S32f.rearrange("p (h v) -> p h v", v=128)
        S32b = [Buf("S32_%d" % i, marena, i * 512, (i + 1) * 512) for i in range(16)]
        SBFf, _ = mr.get("SBF", 16 * 128 * 2); marena.pop()
        SBF = SBFf.rearrange("p (h v) -> p h v", v=128)
        SBFb = [Buf("SBF_%d" % i, marena, 8192 + i * 256, 8192 + (i + 1) * 256) for i in range(16)]

        def mk(name, n, nbytes, dt=BF16):
            aps, bufs = [], []
            for i in range(n):
                a_, b_ = mr.get("%s%d" % (name, i), nbytes, dt)
                aps.append(a_); bufs.append(b_)
            return aps, bufs
        QT, QTb = mk("QT", NG, TT * 2)
        KT, KTb = mk("KT", NG, TT * 2)
        KX, KXb = mk("KX", NG, TT * 2)
        QI, QIb = mk("QI", NG, TT * 2)
        KU, KUb = mk("KU", NG, TT * 2)
        VB, VBb = mk("VB", NG, TT * 2)
        EBE, EBEb = mk("EBE", NG, 16, F32)
        MT, MTb = mk("MT", 3, TT * 4, F32)
        PTl, PTbl = mk("PT", 1, TT * 4, F32); PT, PTb = PTl[0], PTbl[0]
        T1Bl, T1Bbl = mk("T1B", 1, TT * 4, F32); T1B, T1Bb = T1Bl[0], T1Bbl[0]
        AT, ATb = mk("AT", 2, 256)
        KUT, KUTb = mk("KUT", 2, 256)
        O32 = [ga_f32(16384 + s * 2048, TT) for s in range(NG)]
        O32b = [Buf("O32_%d" % s, garena, 16384 + s * 2048, 16384 + (s + 1) * 2048) for s in range(NG)]
        SG = [ga_f32(24576 + s * 2048, TT) for s in range(NG)]
        SGb = [Buf("SG_%d" % s, garena, 24576 + s * 2048, 24576 + (s + 1) * 2048) for s in range(NG)]
        TG = [ga_f32(32768 + s * 2048, TT) for s in range(6)]
        TGb = [Buf("TG_%d" % s, garena, 32768 + s * 2048, 32768 + (s + 1) * 2048) for s in range(6)]
        P.op("dve", lambda e: e.memset(EPSR[:, :], EPS), w=[EPSRb])
    slot_ctr = [0]

    def rec_tile(l, r, ti, w):
        if ti == 0:
            VMEMSET(S32[:, :, :], 0.0, S32b)
            VMEMSET(SBF[:, :, :], 0.0, SBFb)
        bw = 16 if w == NMETA else 128
        nblk = w // bw
        chunks = [(0, 16)] if w == NMETA else [(c * 64, 64) for c in range(w // 64)]
        blocks = [(b * bw, bw) for b in range(nblk)]
        norm_H(l * KC, w)
        T1, T2, T3, T4, T5, T6 = TG
        T1b, T2b, T3b, T4b, T5b, T6b = TGb
        M1, M2, M3 = MT
        M1b, M2b, M3b = MTb
        for g0 in range(0, 16, NG):
            T1x = [T1, T1B]
            T1xb = [T1b, T1Bb]
            M2x = [M2, T6]
            M2xb = [M2b, T6b]

            def build_chain(hd, s):
                lbc = LB[:, r * 16 + hd:r * 16 + hd + 1]
                omc = OML[:, r * 16 + hd:r * 16 + hd + 1]
                t1, t1b = T1x[hd % 2], T1xb[hd % 2]
                m2, m2b = M2x[hd % 2], M2xb[hd % 2]
                ops = []
                ops.append(lambda: ACTF(t1[:, :w], t1[:, :w], AF.Identity, [t1b, OMLb, LBb], [t1b], scale=omc, bias=lbc))
                ops.append(lambda: ACTF(T2[:, :w], t1[:, :w], AF.Ln, [t1b], [T2b]))
                ops.append(lambda: ACTF(T3[:, :w], t1[:, :w], AF.Identity, [t1b, ONECb], [T3b], scale=-1.0,
                                        bias=ONEC[:, 0:1]))
                ops.append(lambda: P.op("dve", lambda e: e.tensor_tensor_scan(
                    out=T4[:, :w], data0=RESET64[:, :w], data1=T2[:, :w], initial=0.0, op0=ALU.mult, op1=ALU.add),
                    r=[CONSb, T2b], w=[T4b]))
                ops.append(lambda: P.op("dve", lambda e: e.tensor_tensor_scan(
                    out=T5[:, :w], data0=RESET128[:, :w], data1=T2[:, :w], initial=0.0, op0=ALU.mult, op1=ALU.add),
                    r=[CONSb, T2b], w=[T5b]))
                ops.append(lambda: ACTF(M1[:, :w], T4[:, :w], AF.Exp, [T4b], [M1b]))
                ops.append(lambda: VTT(QT[s][:, :w], m2[:, :w], M1[:, :w], ALU.mult, [M1b, m2b], [QTb[s]]))
                ops.append(lambda: ACTF(M3[:, :w], T4[:, :w], AF.Exp, [T4b], [M3b], scale=-1.0))
                ops.append(lambda: VTT(KT[s][:, :w], T3[:, :w], M3[:, :w], ALU.mult, [T3b, M3b], [KTb[s]]))
                for (cs, cw) in chunks:
                    ops.append(lambda cs=cs, cw=cw: ACTF(M1[:, cs:cs + cw], T4[:, cs:cs + cw], AF.Exp, [T4b], [M1b],
                                                         scale=-1.0, bias=T4[:, cs + cw - 1:cs + cw]))
                ops.append(lambda: VTT(KX[s][:, :w], T3[:, :w], M1[:, :w], ALU.mult, [T3b, M1b], [KXb[s]]))
                ops.append(lambda: ACTF(M3[:, :w], T5[:, :w], AF.Exp, [T5b], [M3b]))
                ops.append(lambda: VTT(QI[s][:, :w], m2[:, :w], M3[:, :w], ALU.mult, [m2b, M3b], [QIb[s]]))
                for bi, (c0, _) in enumerate(blocks):
                    ops.append(lambda bi=bi, c0=c0: ACOPY(EBE[s][:, bi:bi + 1], M3[:, c0 + bw - 1:c0 + bw], [M3b],
                                                          [EBEb[s]]))
                for (c0, _) in blocks:
                    ops.append(lambda c0=c0: ACTF(M1[:, c0:c0 + bw], T5[:, c0:c0 + bw], AF.Exp, [T5b], [M1b],
                                                  scale=-1.0, bias=T5[:, c0 + bw - 1:c0 + bw]))
                ops.append(lambda: VTT(KU[s][:, :w], T3[:, :w], M1[:, :w], ALU.mult, [T3b, M1b], [KUb[s]]))
                return ops

            prev_chain = None
            for s in range(NG):
                hd = g0 + s
                tiles = []
                for which in range(4):
                    wt, wtb = next_mw()
                    i = which * 16 + hd
                    P.dma("sp", wt[:, :], rwin_b[r][i], r=[rwinb_b[r][i]], w=[wtb])
                    tiles.append((wt, wtb))
                for which, bank in ((0, 0), (1, 1), (3, 2)):
                    wt, wtb = tiles[which]
                    for kc in range(KC):
                        MM(PS[bank][:, :w], wt[:, kc * 128:(kc + 1) * 128], H[:, kc, :w], kc == 0, kc == KC - 1,
                           [wtb, Hb[kc]], [PSb[bank]])
                wt, wtb = tiles[2]
                for bi, (c0, _) in enumerate(blocks):
                    for kc in range(KC):
                        MM(PS[3][:bw, bi * 128:(bi + 1) * 128], H[:, kc, c0:c0 + bw], wt[:, kc * 128:(kc + 1) * 128],
                           kc == 0, kc == KC - 1, [wtb, Hb[kc]], [PSb[3]])
                if prev_chain is not None:
                    ksplit = (3 * len(prev_chain)) // 4
                    for op_ in prev_chain[:ksplit]:
                        op_()
                t1, t1b = T1x[hd % 2], T1xb[hd % 2]
                m2, m2b = M2x[hd % 2], M2xb[hd % 2]
                ACTF(t1[:, :w], PS[1][:, :w], AF.Sigmoid, [PSb[1]], [t1b])
                ACTF(m2[:, :w], PS[0][:, :w], AF.Silu, [PSb[0]], [m2b])
                ACTF(SG[s][:, :w], PS[2][:, :w], AF.Silu, [PSb[2]], [SGb[s]])
                ACOPY(VB[s][:bw, :nblk * 128], PS[3][:bw, :nblk * 128], [PSb[3]], [VBb[s]])
                if prev_chain is not None:
                    for op_ in prev_chain[ksplit:]:
                        op_()
                prev_chain = build_chain(hd, s)
            for op_ in prev_chain:
                op_()
            bsteps = [(bi, c0, s) for bi, (c0, _) in enumerate(blocks) for s in range(NG)]
            bsteps.sort(key=lambda t_: (t_[0] + (1 if t_[2] >= 2 else 0), t_[2] >= 2, t_[0], t_[2]))
            kbase = slot_ctr[0]
            slot_ctr[0] += len(bsteps)

            def rb_front(i):
                bi, c0, s = bsteps[i]
                k = kbase + i
                q4 = (k % 4) * 128
                at, atb = AT[k % 2], ATb[k % 2]
                kut, kutb = KUT[k % 2], KUTb[k % 2]
                sbank = (3, 0)[k % 2]
                st, stb = PS[sbank][:, q4:q4 + 128], PSb[sbank]
                MM(st[:bw, :bw], KT[s][:, c0:c0 + bw], QT[s][:, c0:c0 + bw], True, True, [KTb[s], QTb[s]], [stb])
                if bw == 128:
                    MM(st[0:64, 64:128], KX[s][:, c0:c0 + 64], QT[s][:, c0 + 64:c0 + 128], True, True,
                       [KXb[s], QTb[s]], [stb])
                VTT(at[:bw, :bw], st[:bw, :bw], MASKBD[:bw, :bw], ALU.mult, [stb, CONSb], [atb])
                pst = PST[:, (k % 8) * 128:(k % 8) * 128 + 128]
                P.op("pe", lambda e: e.transpose(out=pst[:bw, :], in_=KU[s][:, c0:c0 + bw], identity=IDENT[:, :]),
                     r=[KUb[s], IDENTb], w=[PSTb])
                ACOPY(kut[:bw, :], pst[:bw, :], [PSTb], [kutb])

            def rb_back(i):
                bi, c0, s = bsteps[i]
                hd = g0 + s
                k = kbase + i
                q4 = (k % 4) * 128
                at, atb = AT[k % 2], ATb[k % 2]
                kut, kutb = KUT[k % 2], KUTb[k % 2]
                obank = (4, 1)[k % 2]
                osl, oslb = PS[obank][:, q4:q4 + 128], PSb[obank]
                MM(osl[:, :bw], VB[s][:bw, bi * 128:(bi + 1) * 128], at[:bw, :bw], True, False, [VBb[s], atb], [oslb])
                MM(osl[:, :bw], SBF[:, hd, :], QI[s][:, c0:c0 + bw], False, True, [SBFb[hd], QIb[s]], [oslb])
                ubank = (5, 2)[k % 2]
                usl, uslb = PS[ubank][:, q4:q4 + 128], PSb[ubank]
                MM(usl[:, :], kut[:bw, :], VB[s][:bw, bi * 128:(bi + 1) * 128], True, True, [kutb, VBb[s]], [uslb])
                VSTT(S32[:, hd, :], S32[:, hd, :], EBE[s][:, bi:bi + 1], usl[:, :], ALU.mult, ALU.add,
                     [S32b[hd], EBEb[s], uslb], [S32b[hd]])
                ACOPY(SBF[:, hd, :], S32[:, hd, :], [S32b[hd]], [SBFb[hd]])
                ACOPY(O32[s][:, c0:c0 + bw], osl[:, :bw], [oslb], [O32b[s]])

            for i in range(len(bsteps)):
                if PIPE_R:
                    if i == 0:
                        rb_front(0)
                    if i + 1 < len(bsteps):
                        rb_front(i + 1)
                else:
                    rb_front(i)
                rb_back(i)
            for s in range(NG):
                hd = g0 + s
                rms_rstd(w, 1.0 / 128, [(O32[s][:, :w], [O32b[s]])], EPSR[:, 0:1], EPSRb)
                VSTT(PT[:, :w], O32[s][:, :w], ONG[:, r:r + 1], RSTD[:, :w], ALU.mult, ALU.mult,
                     [O32b[s], ONGb, RSTDb], [PTb])
                VTT(Gv(hd, w), PT[:, :w], SG[s][:, :w], ALU.mult, [PTb, SGb[s]], [Gb[hd]])
        for j in range(KC):
            wt, wtb = next_mw()
            P.dma("sp", wt[:, :], rwout_b[r][j], r=[rwoutb_b[r][j]], w=[wtb])
            py, pyb = PS[j % 2], PSb[j % 2]
            for kc in range(KC):
                MM(py[:, :w], wt[:, kc * 128:(kc + 1) * 128], Gv(kc, w), kc == 0, kc == KC - 1, [wtb, Gb[kc]], [pyb])
            VTT(X[:, j, :w], X[:, j, :w], py[:, :w], ALU.add, [Xb[j], pyb], [Xb[j]])


    if cfg.n_att > 0:
        ntile = len(cfg.tiles)
        mat = Bump(MA, marena, 0, MA_BYTES)
        QACf, _ = mat.get("QAC", 16384); marena.pop()
        QAC = QACf.rearrange("p (h t) -> p h t", t=TT)
        QACb = [Buf("QAC%d" % h, marena, h * 1024, (h + 1) * 1024) for h in range(16)]
        KAf, KAb = mat.get("KA", 5120); KA = KAf.rearrange("p (j t) -> p j t", t=640)
        VAf, VAb = mat.get("VA", 2560); VA = VAf.rearrange("p (b v) -> p b v", v=256)
        CQNf, CQNb = mat.get("CQN", 4096); CQN = CQNf.rearrange("p (c t) -> p c t", t=TT)
        CKVNf, CKVNb = mat.get("CKVN", 2048); CKVN = CKVNf.rearrange("p (c t) -> p c t", t=TT)
        KL, KLb, VL, VLb = [], [], [], []
        for i in range(2):
            a_, b_ = mat.get("KL%d" % i, 4096); KL.append(a_.rearrange("p (e t) -> p e t", e=2)); KLb.append(b_)
        for i in range(2):
            a_, b_ = mat.get("VL%d" % i, 2048); VL.append(a_.rearrange("p (b v) -> p b v", v=128)); VLb.append(b_)
        PTS, PTSb, PTM, PTMb = [], [], [], []
        for i in range(2):
            a_, b_ = mat.get("PTS%d" % i, 1024); PTS.append(a_); PTSb.append(b_)
        for i in range(3):
            a_, b_ = mat.get("PTM%d" % i, 1024); PTM.append(a_); PTMb.append(b_)
        KROPE, KROPEb = mat.get("KROPE", 1024)
        gat = Bump(GA, garena, 16384, 45056)
        KCS, KCSb, VMS, VMSb = [], [], [], []
        for i in range(2):
            a_, b_ = gat.get("KCS%d" % i, 1024); KCS.append(a_); KCSb.append(b_)
        for i in range(2):
            a_, b_ = gat.get("VMS%d" % i, 2048); VMS.append(a_); VMSb.append(b_)
        ROPEC, ROPECb = gat.get("ROPEC", 2048, F32)
        ROPES, ROPESb = gat.get("ROPES", 2048, F32)
        TMPA, TMPAb = gat.get("TMPA", 2048, F32)
        TMPB, TMPBb = gat.get("TMPB", 2048, F32)
        TD, TDb = gat.get("TD", 2048, F32)
        WX0, WX0b = gat.get("WX0", 4096)
        ESINK = sb("ESINK", [128, NA * 8]); ESINKb = Buf("ESINK")
        ANORM = sb("ANORM", [128, NA * 6]); ANORMb = Buf("ANORM")
        EPSQ = sb("EPSQ", [128, 1]); EPSQb = Buf("EPSQ")
        P.dma("sp", ESINK[:, :], sinkc_d[:, :], w=[ESINKb])
        P.dma("sp", ANORM[:, :], anorm_d[:, :], w=[ANORMb])
        ACTF(ESINK[:, :], ESINK[:, :], AF.Exp, [ESINKb], [ESINKb])
        VMEMSET(EPSQ[:, :], EPS, [EPSQb])
        kcatb = [[[Buf("kcat%d_%d_%d" % (a, t, h)) for h in range(16)] for t in range(ntile)] for a in range(cfg.n_att)]
        vmlb = [[[Buf("vml%d_%d_%d" % (a, t, b)) for b in range(4)] for t in range(ntile)] for a in range(cfg.n_att)]
    actr = [0, 0, 0]

    def att_prep(a):
        rotb, rotbb = WX0, WX0b
        rv2 = rotb.rearrange("p (q m) -> p q m", m=32)
        for gq in range(4):
            wt, wtb = next_mw()
            P.dma("sp", wt[:, 0:1536], wuq_b[a][gq], r=[wuqb_b[a][gq]], w=[wtb])
            wv2 = wt[:, 0:1536].rearrange("p (q m) -> p q m", m=96)
            VTS(rv2[:, gq * 16:(gq + 1) * 16, 0:16], wv2[:, :, 80:96], -1.0, None, ALU.mult, None, [wtb], [rotbb])
            VCOPY(rv2[:, gq * 16:(gq + 1) * 16, 16:32], wv2[:, :, 64:80], [wtb], [rotbb])
        P.dma("sp", wrot_b[a], rotb[:, :], r=[rotbb], w=[wrotb_b[a]])
        wt, wtb = next_mw()
        P.dma("sp", wt[:, 0:512], awkr_b[a], r=[awkrb_b[a]], w=[wtb])
        wr, wrb = next_mw()
        wv3 = wt[:, 0:512].rearrange("p (q m) -> p q m", m=32)
        rv3 = wr[:, 0:512].rearrange("p (q m) -> p q m", m=32)
        VTS(rv3[:, :, 0:16], wv3[:, :, 16:32], -1.0, None, ALU.mult, None, [wtb], [wrb])
        VCOPY(rv3[:, :, 16:32], wv3[:, :, 0:16], [wtb], [wrb])
        P.dma("sp", awkrrot_b[a], wr[:, 0:512], r=[wrb], w=[awkrrotb_b[a]])

    def att_tile(l, a, ti, t0, w):
        bw = 16 if w == NMETA else 128
        nblk = w // bw
        blocks = [(b * bw, bw) for b in range(nblk)]
        pw0 = 0 if ti == 0 else (16 if ti == 1 else 128)
        norm_H(l * KC, w)
        for c in range(8):
            wt, wtb = next_mw()
            P.dma("sp", wt[:, :], awin_b[a][c], r=[awinb_b[a][c]], w=[wtb])
            for e in range(2):
                h = 2 * c + e
                ps, psb = PS[h % 2], PSb[h % 2]
                for kc in range(KC):
                    MM(ps[0:64, :w], wt[:, kc * 128 + e * 64:kc * 128 + e * 64 + 64], H[:, kc, :w], kc == 0,
                       kc == KC - 1, [wtb, Hb[kc]], [psb])
                ACOPY(QAC[0:64, h, :w], ps[0:64, :w], [psb], [QACb[h]])
        for c in range(2):
            wt, wtb = next_mw()
            P.dma("sp", wt[:, :], awin_b[a][8 + c], r=[awinb_b[a][8 + c]], w=[wtb])
            for e in range(2):
                j = 2 * c + e
                ps, psb = PS[j % 2], PSb[j % 2]
                for kc in range(KC):
                    MM(ps[0:64, :w], wt[:, kc * 128 + e * 64:kc * 128 + e * 64 + 64], H[:, kc, :w], kc == 0,
                       kc == KC - 1, [wtb, Hb[kc]], [psb])
                ACOPY(KA[0:64, j, 128:128 + w], ps[0:64, :w], [psb], [KAb])
        wvs = []
        for g in range(2):
            wt, wtb = next_mw()
            P.dma("sp", wt[:, :], awin_b[a][10 + g], r=[awinb_b[a][10 + g]], w=[wtb])
            wvs.append((wt, wtb))
        for bi, (c0, _) in enumerate(blocks):
            bank = 2 + bi // 2
            off = (bi % 2) * 256
            for g, (wt, wtb) in enumerate(wvs):
                for kc in range(KC):
                    MM(PS[bank][:bw, off + g * 128:off + (g + 1) * 128], H[:, kc, c0:c0 + bw],
                       wt[:, kc * 128:(kc + 1) * 128], kc == 0, kc == KC - 1, [wtb, Hb[kc]], [PSb[bank]])
            ACOPY(VA[:bw, 1 + bi, :], PS[bank][:bw, off:off + 256], [PSb[bank]], [VAb])
        def v3(ap, rows, half):
            return ap[:rows, half * 256:(half + 1) * 256].rearrange("p (e q) -> p e q", e=2)[:, :, :bw]

        def m3(mask, rows):
            return mask[:rows, :].rearrange("p (e q) -> p e q", e=2)[:, :, :bw]

        sw = [(c, bi) for c in range(8) for bi in range(nblk)]
        swbase = actr[0]
        actr[0] += len(sw)

        def sw_geo(bi):
            c0 = blocks[bi][0]
            if bi == 0:
                return c0, pw0, 0, 0, 128 + c0, 1 + bi
            return c0, 128, 128 + c0 - 128, bi, 128 + c0, 1 + bi

        def swa_front(i):
            c, bi = sw[i]
            kv = c // 2
            k = swbase + i
            ps, psb = PS[k % 2], PSb[k % 2]
            pt, ptb = PTS[k % 2], PTSb[k % 2]
            c0, pw, pc0, pslot, cc0, cslot = sw_geo(bi)
            for e in range(2):
                h = 2 * c + e
                if pw > 0:
                    MM(ps[:pw, e * 128:e * 128 + bw], KA[0:64, kv, pc0:pc0 + pw], QAC[0:64, h, c0:c0 + bw],
                       True, True, [KAb, QACb[h]], [psb])
                MM(ps[:bw, 256 + e * 128:256 + e * 128 + bw], KA[0:64, kv, cc0:cc0 + bw],
                   QAC[0:64, h, c0:c0 + bw], True, True, [KAb, QACb[h]], [psb])
            if pw > 0:
                ACTF(v3(pt, pw, 0), v3(ps, pw, 0), AF.Exp, [psb], [ptb], scale=0.125)
                mk_ = PREV16_2 if pw == 16 else STRICT2
                VTT(v3(pt, pw, 0), v3(pt, pw, 0), m3(mk_, pw), ALU.mult, [ptb, CONSb], [ptb])
            ACTF(v3(pt, bw, 1), v3(ps, bw, 1), AF.Exp, [psb], [ptb], scale=0.125)
            VTT(v3(pt, bw, 1), v3(pt, bw, 1), m3(TRI2, bw), ALU.mult, [ptb, CONSb], [ptb])

        def swa_back(i):
            c, bi = sw[i]
            kv = c // 2
            k = swbase + i
            pt, ptb = PTS[k % 2], PTSb[k % 2]
            po, pob = PS[2 + 2 * (c % 2)], PSb[2 + 2 * (c % 2)]
            pd, pdb = PS[3 + 2 * (c % 2)], PSb[3 + 2 * (c % 2)]
            c0, pw, pc0, pslot, cc0, cslot = sw_geo(bi)
            for e in range(2):
                rs = slice(e * 64, (e + 1) * 64)
                if pw > 0:
                    MM(po[rs, c0:c0 + bw], VA[:pw, pslot, kv * 64:(kv + 1) * 64], pt[:pw, e * 128:e * 128 + bw],
                       True, False, [VAb, ptb], [pob])
                    MM(pd[rs, c0:c0 + bw], ONES[:pw, 0:64], pt[:pw, e * 128:e * 128 + bw],
                       True, False, [ONESb, ptb], [pdb])
                MM(po[rs, c0:c0 + bw], VA[:bw, cslot, kv * 64:(kv + 1) * 64],
                   pt[:bw, 256 + e * 128:256 + e * 128 + bw], pw == 0, True, [VAb, ptb], [pob])
                MM(pd[rs, c0:c0 + bw], ONES[:bw, 0:64], pt[:bw, 256 + e * 128:256 + e * 128 + bw],
                   pw == 0, True, [ONESb, ptb], [pdb])
            if bi == nblk - 1:
                VTS(TD[:, :w], pd[:, :w], ESINK[:, a * 8 + c:a * 8 + c + 1], None, ALU.add, None, [pdb, ESINKb], [TDb])
                VRECIP(TD[:, :w], TD[:, :w], [TDb], [TDb])
                VTT(Gv(c, w), po[:, :w], TD[:, :w], ALU.mult, [pob, TDb], [Gb[c]])

        for i in range(len(sw)):
            if PIPE_S:
                if i == 0:
                    swa_front(0)
                if i + 1 < len(sw):
                    swa_front(i + 1)
            else:
                swa_front(i)
            swa_back(i)
        cwid = min(w, 128)
        ACOPY(KA[0:64, :, 0:cwid], KA[0:64, :, 128 + w - cwid:128 + w], [KAb], [KAb])
        ACOPY(VA[:cwid, 0, :], VA[:cwid, nblk, :], [VAb], [VAb])
        for c in range(2):
            wt, wtb = next_mw()
            P.dma("sp", wt[:, :], awin_b[a][16 + c], r=[awinb_b[a][16 + c]], w=[wtb])
            for kc in range(KC):
                MM(PS[4 + c][:, :w], wt[:, kc * 128:(kc + 1) * 128], H[:, kc, :w], kc == 0, kc == KC - 1,
                   [wtb, Hb[kc]], [PSb[4 + c]])
        rms_rstd(w, 1.0 / 256, [(PS[4 + c][:, :w], [PSb[4 + c]]) for c in range(2)], EPSQ[:, 0:1], EPSQb)
        for c in range(2):
            VSTT(CKVN[:, c, :w], PS[4 + c][:, :w], ANORM[:, a * 6 + 4 + c:a * 6 + 5 + c], RSTD[:, :w], ALU.mult,
                 ALU.mult, [PSb[4 + c], ANORMb, RSTDb], [CKVNb])
        for c in range(4):
            wt, wtb = next_mw()
            P.dma("sp", wt[:, :], awin_b[a][12 + c], r=[awinb_b[a][12 + c]], w=[wtb])
            for kc in range(KC):
                MM(PS[c][:, :w], wt[:, kc * 128:(kc + 1) * 128], H[:, kc, :w], kc == 0, kc == KC - 1,
                   [wtb, Hb[kc]], [PSb[c]])
        rms_rstd(w, 1.0 / 512, [(PS[c][:, :w], [PSb[c]]) for c in range(4)], EPSQ[:, 0:1], EPSQb)
        for c in range(4):
            VSTT(CQN[:, c, :w], PS[c][:, :w], ANORM[:, a * 6 + c:a * 6 + c + 1], RSTD[:, :w], ALU.mult,
                 ALU.mult, [PSb[c], ANORMb, RSTDb], [CQNb])
        P.dma("sp", ROPEC[64:96, :w], ropec_d[:, t0:t0 + w], w=[ROPECb])
        P.dma("sp", ROPES[64:96, :w], ropes_d[:, t0:t0 + w], w=[ROPESb])
        wkr, wkrb = next_mw()
        P.dma("sp", wkr[:, 0:512], awkr_b[a], r=[awkrb_b[a]], w=[wkrb])
        wkrr, wkrrb = next_mw()
        P.dma("sp", wkrr[:, 0:512], awkrrot_b[a], r=[awkrrotb_b[a]], w=[wkrrb])
        for kc in range(KC):
            MM(PS[4][64:96, :w], wkr[:, kc * 32:(kc + 1) * 32], H[:, kc, :w], kc == 0, kc == KC - 1,
               [wkrb, Hb[kc]], [PSb[4]])
        for kc in range(KC):
            MM(PS[5][64:96, :w], wkrr[:, kc * 32:(kc + 1) * 32], H[:, kc, :w], kc == 0, kc == KC - 1,
               [wkrrb, Hb[kc]], [PSb[5]])
        VTT(TMPA[64:96, :w], PS[4][64:96, :w], ROPEC[64:96, :w], ALU.mult, [PSb[4], ROPECb], [TMPAb])
        VTT(TMPB[64:96, :w], PS[5][64:96, :w], ROPES[64:96, :w], ALU.mult, [PSb[5], ROPESb], [TMPBb])
        VTT(KROPE[64:96, :w], TMPA[64:96, :w], TMPB[64:96, :w], ALU.add, [TMPAb, TMPBb], [KROPEb])
        P.dma("sp", WX0[:, :], wrot_b[a], r=[wrotb_b[a]], w=[WX0b])
        for gq in range(4):
            wt, wtb = next_mw()
            P.dma("sp", wt[:, 0:1536], wuq_b[a][gq], r=[wuqb_b[a][gq]], w=[wtb])
            for hh in range(4):
                h = gq * 4 + hh
                pm, pmb = PS[2 * (h % 2)], PSb[2 * (h % 2)]
                pr, prb = PS[1 + 2 * (h % 2)], PSb[1 + 2 * (h % 2)]
                for kc in range(4):
                    MM(pm[0:96, :w], wt[:, hh * 384 + kc * 96:hh * 384 + (kc + 1) * 96], CQN[:, kc, :w],
                       kc == 0, kc == 3, [wtb, CQNb], [pmb])
                for kc in range(4):
                    MM(pr[64:96, :w], WX0[:, h * 128 + kc * 32:h * 128 + (kc + 1) * 32], CQN[:, kc, :w],
                       kc == 0, kc == 3, [WX0b, CQNb], [prb])
                ACOPY(QAC[0:64, h, :w], pm[0:64, :w], [pmb], [QACb[h]])
                VTT(TMPA[64:96, :w], pm[64:96, :w], ROPEC[64:96, :w], ALU.mult, [pmb, ROPECb], [TMPAb])
                VTT(TMPB[64:96, :w], pr[64:96, :w], ROPES[64:96, :w], ALU.mult, [prb, ROPESb], [TMPBb])
                VTT(QAC[64:96, h, :w], TMPA[64:96, :w], TMPB[64:96, :w], ALU.add, [TMPAb, TMPBb], [QACb[h]])
        wk, wkb = next_mw()
        P.dma("sp", wk[:, :], wukvk_b[a], r=[wukvkb_b[a]], w=[wkb])
        for h in range(16):
            c, e = h // 2, h % 2
            ps, psb = PS[4 + h % 2], PSb[4 + h % 2]
            for kc in range(2):
                MM(ps[0:64, :w], wk[:, c * 256 + kc * 128 + e * 64:c * 256 + kc * 128 + e * 64 + 64],
                   CKVN[:, kc, :w], kc == 0, kc == 1, [wkb, CKVNb], [psb])
            kcs, kcsb = KCS[h % 2], KCSb[h % 2]
            ACOPY(kcs[0:64, :w], ps[0:64, :w], [psb], [kcsb])
            ACOPY(kcs[64:96, :w], KROPE[64:96, :w], [KROPEb], [kcsb])
            P.dma("sp", kcat_s[a][h, :, t0:t0 + w], kcs[0:96, :w], r=[kcsb], w=[kcatb[a][ti][h]])
        wv, wvb = next_mw()
        P.dma("sp", wv[:, :], wukvv_b[a], r=[wukvvb_b[a]], w=[wvb])
        for bi, (c0, _) in enumerate(blocks):
            vms, vmsb = VMS[bi % 2], VMSb[bi % 2]
            for j in range(2):
                ps, psb = PS[2 * (bi % 2) + j], PSb[2 * (bi % 2) + j]
                for kc in range(2):
                    MM(ps[:bw, :], CKVN[:, kc, c0:c0 + bw], wv[:, j * 1024 + kc * 512:j * 1024 + (kc + 1) * 512],
                       kc == 0, kc == 1, [wvb, CKVNb], [psb])
                ACOPY(vms[:bw, j * 512:(j + 1) * 512], ps[:bw, :], [psb], [vmsb])
            P.dma("sp", vml_s[a][t0 + c0:t0 + c0 + bw, :], vms[:bw, :], r=[vmsb], w=[vmlb[a][ti][bi]])
        SC = 96.0 ** -0.5
        kblocks = [(0, 16, 0, 0)]
        for tj in range(1, ti + 1):
            for bj in range(4):
                kblocks.append((NMETA + (tj - 1) * TT + bj * 128, 128, tj, bj))
        groups = [kblocks[i:i + 8] for i in range(0, len(kblocks), 8)]
        nkb = len(kblocks)
        flat = []
        for c in range(8):
            idx = 0
            for grp in groups:
                for si, blk in enumerate(grp):
                    for e in range(2):
                        flat.append(dict(c=c, grp=grp, si=si, blk=blk, e=e, idx=idx, first=(si == 0 and e == 0)))
                    idx += 1
        mbase = actr[2]
        actr[2] += len(flat)
        gstate = {}
        gorder = []
        for st_ in flat:
            if st_["first"]:
                gorder.append((st_["c"], st_["grp"]))
        gpos = {(c_, id(g_)): n_ for n_, (c_, g_) in enumerate(gorder)}

        def load_group(c, grp):
            g_ = actr[1]; actr[1] += 1
            kl, klb = KL[g_ % 2], KLb[g_ % 2]
            vl, vlb = VL[g_ % 2], VLb[g_ % 2]
            ks, ke = grp[0][0], grp[-1][0] + grp[-1][1]
            tjs = sorted(set(b[2] for b in grp))
            for e in range(2):
                P.dma("sp", kl[0:96, e, 0:ke - ks], kcat_s[a][2 * c + e, :, ks:ke],
                      r=[kcatb[a][tj][2 * c + e] for tj in tjs], w=[klb])
            si0 = 0
            if grp[0][1] == 16:
                P.dma("sp", vl[:16, 0, :], vml_s[a][0:16, c * 128:(c + 1) * 128], r=[vmlb[a][0][0]], w=[vlb])
                si0 = 1
            if len(grp) > si0:
                kf = grp[si0][0]
                nb = len(grp) - si0
                P.dma("sp", vl[:, si0:si0 + nb, :],
                      vml_s[a][kf:kf + nb * 128, c * 128:(c + 1) * 128].rearrange("(b p) v -> p b v", p=128),
                      r=[vmlb[a][b[2]][b[3]] for b in grp[si0:]], w=[vlb])
            return kl, klb, vl, vlb, ks

        def mla_front(i):
            st = flat[i]
            c, grp, si, e = st["c"], st["grp"], st["si"], st["e"]
            if st["first"] and (c, id(grp)) not in gstate:
                gstate[(c, id(grp))] = load_group(c, grp)
            kl, klb, vl, vlb, ks = gstate[(c, id(grp))]
            k0, kw, tj, bj = st["blk"]
            diag = (tj == ti)
            q0 = bj * 128 if (diag and ti > 0) else 0
            h = 2 * c + e
            k = mbase + i
            bsel = (0, 1, 6)[k % 3]
            ps, psb = PS[bsel], PSb[bsel]
            pt, ptb = PTM[k % 3], PTMb[k % 3]
            MM(ps[:kw, q0:w], kl[0:96, e, k0 - ks:k0 - ks + kw], QAC[0:96, h, q0:w], True, True,
               [klb, QACb[h]], [psb])
            ACTF(pt[:kw, q0:w], ps[:kw, q0:w], AF.Exp, [psb], [ptb], scale=SC)
            if diag:
                VTT(pt[:kw, q0:q0 + bw], pt[:kw, q0:q0 + bw], TRI2[:kw, 0:bw], ALU.mult, [ptb, CONSb], [ptb])

        def mla_back(i):
            st = flat[i]
            c, grp, si, e = st["c"], st["grp"], st["si"], st["e"]
            kl, klb, vl, vlb, ks = gstate[(c, id(grp))]
            k0, kw, tj, bj = st["blk"]
            diag = (tj == ti)
            q0 = bj * 128 if (diag and ti > 0) else 0
            k = mbase + i
            pt, ptb = PTM[k % 3], PTMb[k % 3]
            po, pob = PS[2 + 2 * (c % 2)], PSb[2 + 2 * (c % 2)]
            pd, pdb = PS[3 + 2 * (c % 2)], PSb[3 + 2 * (c % 2)]
            rs = slice(e * 64, (e + 1) * 64)
            MM(po[rs, q0:w], vl[:kw, si, e * 64:(e + 1) * 64], pt[:kw, q0:w], st["idx"] == 0, st["idx"] == nkb - 1,
               [vlb, ptb], [pob])
            MM(pd[rs, q0:w], ONES[:kw, 0:64], pt[:kw, q0:w], st["idx"] == 0, st["idx"] == nkb - 1,
               [ONESb, ptb], [pdb])
            if st["first"]:
                n_ = gpos[(c, id(grp))] + 1
                if n_ < len(gorder):
                    c2, g2 = gorder[n_]
                    if (c2, id(g2)) not in gstate:
                        gstate[(c2, id(g2))] = load_group(c2, g2)
            if st["idx"] == nkb - 1 and e == 1:
                VRECIP(TD[:, :w], pd[:, :w], [pdb], [TDb])
                VTT(Gv(8 + c, w), po[:, :w], TD[:, :w], ALU.mult, [pob, TDb], [Gb[8 + c]])

        LA = 2
        for i in range(len(flat)):
            if PIPE_M:
                if i == 0:
                    for j_ in range(min(LA, len(flat))):
                        mla_front(j_)
                if i + LA < len(flat):
                    mla_front(i + LA)
            else:
                mla_front(i)
            mla_back(i)
        for j in range(KC):
            wt, wtb = next_mw()
            P.dma("sp", wt[:, :], awout_b[a][j], r=[awoutb_b[a][j]], w=[wtb])
            py, pyb = PS[j % 2], PSb[j % 2]
            for kc in range(KC):
                MM(py[:, :w], wt[:, kc * 128:(kc + 1) * 128], Gv(kc, w), kc == 0, kc == KC - 1, [wtb, Gb[kc]], [pyb])
            VTT(X[:, j, :w], X[:, j, :w], py[:, :w], ALU.add, [Xb[j], pyb], [Xb[j]])

    hs_v = hsT.rearrange("(kc p) t -> p kc t", p=128)
    x_v = xT.rearrange("(kc p) t -> p kc t", p=128)
    m_v = metaT.rearrange("(kc p) t -> p kc t", p=128)
    o_v = outT.rearrange("(kc p) t -> p kc t", p=128)

    XG = [(0, 4), (4, 8), (8, 12), (12, 14), (14, 15), (15, 16)]
    hs_b = [[Buf("hs%d_%d" % (i, q)) for q in range(len(XG))] for i in range(len(cfg.tiles))]
    def emit_xload(l, ti, q):
        t0, w = cfg.tiles[ti]
        k0_, k1_ = XG[q]
        ks = slice(k0_, k1_)
        if l == 0:
            src = m_v[:, ks, :] if ti == 0 else x_v[:, ks, t0 - NMETA:t0 - NMETA + w]
            P.dma("sp", X[:, ks, :w], src, w=Xb[k0_:k1_])
        else:
            P.dma("sp", X[:, ks, :w], hs_v[:, ks, t0:t0 + w], r=[hs_b[ti][q]], w=Xb[k0_:k1_])

    seq = [(l, ti) for l in range(L) for ti in range(len(cfg.tiles))]
    early = set()
    ri = 0
    ai = 0
    for idx, (l, ti) in enumerate(seq):
        lt = cfg.layers[l]
        t0, w = cfg.tiles[ti]
        if True:
            if (l, ti) not in early:
                for q in range(len(XG)):
                    emit_xload(l, ti, q)
            if l + 1 < L and PACE_ON:
                nt_ = len(cfg.tiles)
                nj = len(cast_jobs[l + 1])
                lo_, hi_ = (nj * ti) // nt_, (nj * (ti + 1)) // nt_
                if hi_ > lo_:
                    pb = Buf("pace%d_%d" % (l, ti))
                    VMEMSET(PACE[:, :], 0.0, [pb])
                    emit_casts(l + 1, lo_, hi_, pb)
            if lt == "r":
                rec_tile(l, ri, ti, w)
            if lt == "a":
                if ti == 0:
                    att_prep(ai)
                att_tile(l, ai, ti, t0, w)
            if l < L - 1:
                nxt = seq[idx + 1] if idx + 1 < len(seq) else None
                if nxt is not None:
                    early.add(nxt)

                def store_chunk(j, ti=ti, t0=t0, w=w, nxt=nxt):
                    for q, (k0_, k1_) in enumerate(XG):
                        if j == k1_ - 1:
                            ks = slice(k0_, k1_)
                            P.dma("sp", hs_v[:, ks, t0:t0 + w], X[:, ks, :w], r=Xb[k0_:k1_], w=[hs_b[ti][q]])
                            if nxt is not None:
                                emit_xload(nxt[0], nxt[1], q)
                ffn_tile(l, w, store_chunk)
            else:
                ffn_tile(l, w)
                if ti > 0:
                    gc = 2 * L * KC
                    rms_rstd(w, 1.0 / D, [(X[:, kc, :w], [Xb[kc]]) for kc in range(KC)], EPSC[:, 0:1], EPSCb)
                    for kc in range(KC):
                        VSTT(X[:, kc, :w], X[:, kc, :w], GN[:, gc + kc:gc + kc + 1], RSTD[:, :w], ALU.mult, ALU.mult,
                             [Xb[kc], GNb, RSTDb], [Xb[kc]])
                        if kc % 4 == 3:
                            q = kc // 4
                            ks = slice(q * 4, q * 4 + 4)
                            P.dma("sp", o_v[:, ks, t0 - NMETA:t0 - NMETA + w], X[:, ks, :w], r=Xb[q * 4:q * 4 + 4],
                                  w=[Buf("o")], is_out=True)
        if ti == len(cfg.tiles) - 1:
            if lt == "r":
                ri += 1
            if lt == "a":
                ai += 1
    P.finish()
    return nc


def prep_shared(inp, cfg):
    L = cfg.depth
    sh = {}
    gains = [col_layout(np.asarray(inp["mix_norm"][l])) for l in range(L)]
    gains += [col_layout(np.asarray(inp["ffn_norm"][l])) for l in range(L)]
    gains += [col_layout(np.asarray(inp["final_norm"]))]
    sh["gains"] = np.ascontiguousarray(np.concatenate(gains, axis=1), dtype=np.float32)
    cw = np.zeros((128, L, FC, 4), np.float32)
    for l in range(L):
        for j in range(3):
            cw[:, l, :, j] = col_layout(np.asarray(inp["ffn_conv_w"][l, j]))
        cw[:, l, :, 3] = col_layout(np.asarray(inp["ffn_conv_b"][l]))
    sh["convw"] = cw.reshape(128, L * FC * 4)
    for l in range(L):
        sh["wu%d" % l] = tile_w(np.asarray(inp["ffn_w_up"][l]), 128).reshape(FC, 128, KC * 128)
        sh["wg%d" % l] = tile_w(np.asarray(inp["ffn_w_gate"][l]), 128).reshape(FC, 128, KC * 128)
        sh["wd%d" % l] = tile_w(np.asarray(inp["ffn_w_down"][l]), 128).reshape(KC, 128, FC * 128)
    sh["metaT"] = np.ascontiguousarray(np.asarray(inp["meta_tokens"]).T)
    s_i = np.arange(128)[:, None]
    t_i = np.arange(128)[None, :]
    maskbd = (((s_i // 64) == (t_i // 64)) & (s_i <= t_i)) | ((s_i < 64) & (t_i >= 64))
    ident = np.eye(128)
    col = np.arange(512)[None, :]
    reset64 = np.broadcast_to((col % 64 != 0), (128, 512))
    reset128 = np.broadcast_to((col % 128 != 0), (128, 512))
    tri = (s_i <= t_i)
    strict = (s_i > t_i)
    prev16 = np.zeros((128, 128), bool)
    prev16[0:16, :] = (t_i < 112 + np.arange(16)[:, None])
    sh["consts"] = np.ascontiguousarray(np.concatenate(
        [maskbd, ident, reset64, reset128, tri, tri, strict, strict, prev16, prev16], axis=1).astype(np.float32))
    NA = max(cfg.n_att, 1)
    anorm = np.zeros((128, NA * 6), np.float32)
    sinkc = np.zeros((128, NA * 8), np.float32)
    for a in range(cfg.n_att):
        w_in = np.asarray(inp["att_w_in"][a])
        tl = [tile_w(w_in[:, 0:1024], 128), tile_w(w_in[:, 1024:1280], 128), tile_w(w_in[:, 1280:1536], 128),
              tile_w(w_in[:, 1536:2048], 128), tile_w(w_in[:, 2048:2304], 128)]
        sh["awin%d" % a] = np.ascontiguousarray(np.concatenate(tl, axis=0).reshape(18, 128, KC * 128))
        sh["awkr%d" % a] = tile_w(w_in[:, 2304:2336], 32).reshape(128, KC * 32)
        wuq = tile_w(np.asarray(inp["mla_w_uq"][a]), 96)
        sh["wuq%d" % a] = np.ascontiguousarray(
            wuq.reshape(4, 4, 128, 384).transpose(0, 2, 1, 3).reshape(4, 128, 1536))
        wukv = np.asarray(inp["mla_w_ukv"][a]).reshape(256, 16, 128)
        wk_ = np.ascontiguousarray(wukv[:, :, 0:64].reshape(256, 1024))
        wv_ = np.ascontiguousarray(wukv[:, :, 64:128].reshape(256, 1024))
        sh["wukvk%d" % a] = np.ascontiguousarray(tile_w(wk_, 128).transpose(1, 0, 2, 3).reshape(128, 2048))
        sh["wukvv%d" % a] = np.ascontiguousarray(tile_w(wv_, 512).transpose(1, 0, 2, 3).reshape(128, 2048))
        sh["awout%d" % a] = tile_w(np.asarray(inp["att_w_out"][a]), 128).reshape(KC, 128, KC * 128)
        anorm[:, a * 6:a * 6 + 4] = col_layout(np.asarray(inp["mla_q_norm"][a]))
        anorm[:, a * 6 + 4:a * 6 + 6] = col_layout(np.asarray(inp["mla_kv_norm"][a]))
        sk = np.asarray(inp["att_sinks"][a])
        sinkc[0:64, a * 8:(a + 1) * 8] = sk[0::2][None, :]
        sinkc[64:128, a * 8:(a + 1) * 8] = sk[1::2][None, :]
    sh["anorm"] = anorm
    sh["sinkc"] = sinkc
    half = 16
    inv_freq = (np.float32(10000.0) ** (np.float32(-2.0) * np.arange(half, dtype=np.float32) / np.float32(32))).astype(np.float32)
    ang = (np.arange(cfg.T).astype(np.float32)[:, None] * inv_freq[None, :]).astype(np.float32)
    cs, sn = np.cos(ang).astype(np.float32), np.sin(ang).astype(np.float32)
    sh["ropec"] = np.ascontiguousarray(np.concatenate([cs, cs], axis=1).T)
    sh["ropes"] = np.ascontiguousarray(np.concatenate([sn, sn], axis=1).T)
    NR = max(cfg.n_rec, 1)
    lbraw = np.zeros((128, NR * 16), np.float32)
    ong = np.zeros((128, NR), np.float32)
    for r in range(cfg.n_rec):
        sh["rwin%d" % r] = tile_w(np.asarray(inp["rec_w_in"][r]), 128).reshape(64, 128, KC * 128)
        sh["rwout%d" % r] = tile_w(np.asarray(inp["rec_w_out"][r]), 128).reshape(KC, 128, KC * 128)
        lbraw[:, r * 16:(r + 1) * 16] = col_layout(np.asarray(inp["rec_lower_bounds"][r]))
        ong[:, r] = np.asarray(inp["rec_out_norm"][r])
    sh["lbraw"] = lbraw
    sh["ong"] = ong
    return sh


def run(inp, cfg, ncores=8):
    x = np.asarray(inp["x"])
    B = x.shape[0]
    sh = prep_shared(inp, cfg)
    nc = build(cfg)
    if ncores == 8 and B == 4:
        active = [0, 1, 4, 5]
    else:
        active = list(range(min(B, ncores)))
    zero = None
    in_maps = []
    for c in range(ncores):
        if c in active:
            m = dict(sh)
            m["xT"] = np.ascontiguousarray(x[active.index(c)].T)
        else:
            if zero is None:
                zero = {k: np.zeros_like(v) for k, v in sh.items()}
                zero["xT"] = np.zeros((D, cfg.t_real), np.float32)
            m = zero
        in_maps.append(m)
    res = run_bass_kernel_spmd(nc, in_maps, core_ids=list(range(ncores)))
    out = np.stack([np.ascontiguousarray(res.results[active[b]]["outT"].T) for b in range(len(active))], axis=0)
    return out.astype(np.float32)


def kernel(**inputs):
    cfg = Cfg(4096, 4)
    return run(inputs, cfg, ncores=8)
```
